# Optimizing a Trainium2 kernel written in Bass

```python
import math
import jax, jax.numpy as jnp
from jax import lax
import numpy as np

D_MODEL = 1024
BATCH = 32
SEQ = 2048
DEPTH = 1

N_ATT_HEADS = 4
HEAD_DIM = 64
V_HEAD_DIM = 2 * HEAD_DIM
QK_WIDTH = N_ATT_HEADS * 2 * HEAD_DIM
ATT_WIDTH = N_ATT_HEADS * V_HEAD_DIM
POOL_WIDTH = D_MODEL // 2
POOL_WINDOWS = (2, 4, 8, 16)
N_POOL_GROUPS = len(POOL_WINDOWS)
POOL_GROUP_DIM = POOL_WIDTH // N_POOL_GROUPS
N_BRANCHES = 2
IN_WIDTH = 2 * QK_WIDTH + ATT_WIDTH + POOL_WIDTH + N_BRANCHES * D_MODEL
D_FF = 2816
ROPE_THETA = 10000.0
Q_BLOCK = 128
LN_EPS = 1e-5
RMS_EPS = 1e-5
DEEPNORM_ALPHA = (2.0 * DEPTH) ** 0.25
DEEPNORM_BETA = (8.0 * DEPTH) ** -0.25

kernel_name = "hybrid_diffattn_multiscale_pool_macaron_deepnorm"


def lambda_init_for(layer_idx):
    return 0.8 - 0.6 * math.exp(-0.3 * layer_idx)


def layer_norm(x, g, b):
    xf = x.astype(jnp.float32)
    mu = jnp.mean(xf, axis=-1, keepdims=True)
    var = jnp.mean(jnp.square(xf - mu), axis=-1, keepdims=True)
    return ((xf - mu) * lax.rsqrt(var + LN_EPS)).astype(x.dtype) * g + b


def swiglu(x, w_gate, w_up, w_down):
    return (jax.nn.silu(x @ w_gate) * (x @ w_up)) @ w_down


def rope_tables(seq, dtype):
    inv = 1.0 / (ROPE_THETA ** (jnp.arange(0, HEAD_DIM, 2, dtype=jnp.float32) / HEAD_DIM))
    ang = jnp.arange(seq, dtype=jnp.float32)[:, None] * inv[None, :]
    return jnp.cos(ang).astype(dtype), jnp.sin(ang).astype(dtype)


def apply_rope(t, cos, sin):
    t1, t2 = jnp.split(t, 2, axis=-1)
    c = cos[None, :, None, None, :]
    s = sin[None, :, None, None, :]
    return jnp.concatenate([t1 * c - t2 * s, t1 * s + t2 * c], axis=-1)


def diff_attention(q, k, v, lam):
    B, S, H, _, Dh = q.shape
    nb = S // Q_BLOCK
    scale = Dh ** -0.5
    qb = q.reshape(B, nb, Q_BLOCK, H, 2, Dh).transpose(1, 0, 2, 3, 4, 5)

    def block(q_blk):
        s = jnp.einsum('bqhmd,bkhmd->bhmqk', q_blk, k,
                       preferred_element_type=jnp.float32) * scale
        p = jax.nn.softmax(s, axis=-1)
        p_diff = p[:, :, 0] - lam * p[:, :, 1]
        return jnp.einsum('bhqk,bkhe->bqhe', p_diff.astype(v.dtype), v)

    o = lax.map(block, qb)
    return o.transpose(1, 0, 2, 3, 4).reshape(B, S, H, v.shape[-1])


def multiscale_pool(u, pool_w, pool_scale):
    B, S, _ = u.shape
    ug = u.reshape(B, S, N_POOL_GROUPS, POOL_GROUP_DIM)
    cs = jnp.cumsum(ug.astype(jnp.float32), axis=1)
    cs = jnp.concatenate([jnp.zeros_like(cs[:, :1]), cs], axis=1)
    pos = jnp.arange(S)
    means = []
    for g, w in enumerate(POOL_WINDOWS):
        lo = jnp.clip(pos - w // 2, 0, S)
        hi = jnp.clip(pos + (w - w // 2), 0, S)
        cnt = (hi - lo).astype(jnp.float32)[None, :, None]
        means.append((cs[:, hi, g] - cs[:, lo, g]) / cnt)
    pooled = jnp.stack(means, axis=2).astype(u.dtype) - ug
    mixed = jnp.einsum('bsgc,gcd->bsgd', pooled, pool_w)
    return mixed.reshape(B, S, POOL_WIDTH) * pool_scale


def hybrid_mixer(x, w_in, lambda_q1, lambda_k1, lambda_q2, lambda_k2, attn_subln_g,
                 pool_w, pool_scale, w_branch_att, w_branch_pool, w_out,
                 lambda_init, cos, sin):
    B, S, _ = x.shape
    h = x @ w_in
    q, k, v, u, gate_logits = jnp.split(
        h, [QK_WIDTH, 2 * QK_WIDTH, 2 * QK_WIDTH + ATT_WIDTH,
            2 * QK_WIDTH + ATT_WIDTH + POOL_WIDTH], axis=-1)

    q = apply_rope(q.reshape(B, S, N_ATT_HEADS, 2, HEAD_DIM), cos, sin)
    k = apply_rope(k.reshape(B, S, N_ATT_HEADS, 2, HEAD_DIM), cos, sin)
    v = v.reshape(B, S, N_ATT_HEADS, V_HEAD_DIM)
    lam = (jnp.exp(jnp.sum(lambda_q1.astype(jnp.float32) * lambda_k1.astype(jnp.float32)))
           - jnp.exp(jnp.sum(lambda_q2.astype(jnp.float32) * lambda_k2.astype(jnp.float32)))
           + lambda_init)
    o = diff_attention(q, k, v, lam).astype(jnp.float32)
    o = o * lax.rsqrt(jnp.mean(jnp.square(o), axis=-1, keepdims=True) + RMS_EPS)
    o = (o * (1.0 - lambda_init)).astype(x.dtype) * attn_subln_g
    y_att = o.reshape(B, S, ATT_WIDTH)

    y_pool = multiscale_pool(u, pool_w, pool_scale)

    g_att, g_pool = jnp.split(jax.nn.sigmoid(gate_logits), N_BRANCHES, axis=-1)
    merged = g_att * (y_att @ w_branch_att) + g_pool * (y_pool @ w_branch_pool)
    return merged @ w_out


def setup_inputs(seed: int = 0) -> dict:
    key = jax.random.key(seed)
    ks = jax.random.split(key, 24)
    f32 = jnp.float32
    L = DEPTH

    def dense(k, shape, fan_in, scale=1.0):
        return jax.random.normal(k, shape, f32) * (fan_in ** -0.5) * scale

    def gain(k, shape):
        return 1.0 + 0.05 * jax.random.normal(k, shape, f32)

    def bias(k, shape):
        return 0.02 * jax.random.normal(k, shape, f32)

    return {
        "x": jax.random.normal(ks[0], (BATCH, SEQ, D_MODEL), f32),
        "ln1_g": gain(ks[1], (L, D_MODEL)),
        "ln1_b": bias(ks[2], (L, D_MODEL)),
        "ffn1_w_gate": dense(ks[3], (L, D_MODEL, D_FF), D_MODEL),
        "ffn1_w_up": dense(ks[4], (L, D_MODEL, D_FF), D_MODEL),
        "ffn1_w_down": dense(ks[5], (L, D_FF, D_MODEL), D_FF, DEEPNORM_BETA),
        "w_in": dense(ks[6], (L, D_MODEL, IN_WIDTH), D_MODEL),
        "lambda_q1": 0.1 * jax.random.normal(ks[7], (L, HEAD_DIM), f32),
        "lambda_k1": 0.1 * jax.random.normal(ks[8], (L, HEAD_DIM), f32),
        "lambda_q2": 0.1 * jax.random.normal(ks[9], (L, HEAD_DIM), f32),
        "lambda_k2": 0.1 * jax.random.normal(ks[10], (L, HEAD_DIM), f32),
        "attn_subln_g": gain(ks[11], (L, V_HEAD_DIM)),
        "pool_w": dense(ks[12], (L, N_POOL_GROUPS, POOL_GROUP_DIM, POOL_GROUP_DIM), POOL_GROUP_DIM),
        "pool_scale": gain(ks[13], (L, POOL_WIDTH)),
        "w_branch_att": dense(ks[14], (L, ATT_WIDTH, D_MODEL), ATT_WIDTH),
        "w_branch_pool": dense(ks[15], (L, POOL_WIDTH, D_MODEL), POOL_WIDTH),
        "w_out": dense(ks[16], (L, D_MODEL, D_MODEL), D_MODEL, DEEPNORM_BETA),
        "ln2_g": gain(ks[17], (L, D_MODEL)),
        "ln2_b": bias(ks[18], (L, D_MODEL)),
        "ffn2_w_gate": dense(ks[19], (L, D_MODEL, D_FF), D_MODEL),
        "ffn2_w_up": dense(ks[20], (L, D_MODEL, D_FF), D_MODEL),
        "ffn2_w_down": dense(ks[21], (L, D_FF, D_MODEL), D_FF, DEEPNORM_BETA),
        "ln3_g": gain(ks[22], (L, D_MODEL)),
        "ln3_b": bias(ks[23], (L, D_MODEL)),
    }


def reference(x, ln1_g, ln1_b, ffn1_w_gate, ffn1_w_up, ffn1_w_down, w_in,
              lambda_q1, lambda_k1, lambda_q2, lambda_k2, attn_subln_g,
              pool_w, pool_scale, w_branch_att, w_branch_pool, w_out,
              ln2_g, ln2_b, ffn2_w_gate, ffn2_w_up, ffn2_w_down, ln3_g, ln3_b):
    cos, sin = rope_tables(x.shape[1], x.dtype)
    for l in range(DEPTH):
        x = layer_norm(DEEPNORM_ALPHA * x
                       + 0.5 * swiglu(x, ffn1_w_gate[l], ffn1_w_up[l], ffn1_w_down[l]),
                       ln1_g[l], ln1_b[l])
        mix = hybrid_mixer(x, w_in[l], lambda_q1[l], lambda_k1[l], lambda_q2[l], lambda_k2[l],
                           attn_subln_g[l], pool_w[l], pool_scale[l], w_branch_att[l],
                           w_branch_pool[l], w_out[l], lambda_init_for(l), cos, sin)
        x = layer_norm(DEEPNORM_ALPHA * x + mix, ln2_g[l], ln2_b[l])
        x = layer_norm(DEEPNORM_ALPHA * x
                       + 0.5 * swiglu(x, ffn2_w_gate[l], ffn2_w_up[l], ffn2_w_down[l]),
                       ln3_g[l], ln3_b[l])
    return x
```

```python
import math
from contextlib import ExitStack

import numpy as np
import concourse.bass as bass
import concourse.mybir as mybir
from concourse.bass_utils import run_bass_kernel_spmd

F32 = mybir.dt.float32
BF16 = mybir.dt.bfloat16
AF = mybir.ActivationFunctionType
ALU = mybir.AluOpType
AX = mybir.AxisListType

NCORES = 8
D = 1024
DC = 8
FF = 2816
FC = 22
G = 11
S = 2048
NBLK = 16
NSEQ = 4
ALPHA = 2.0 ** 0.25
INV_ALPHA = 1.0 / ALPHA
EPS_LN = 1e-5 / (ALPHA * ALPHA)
EPS_RMS = 1e-5
LAMBDA_INIT = 0.8 - 0.6 * math.exp(-0.3 * 0)
UW = 2072
UOFF = 16
WSLOT = 4224
CONV_BATCH = 8


class Op:
    __slots__ = ("eng", "fn", "deps", "signal", "tick", "dma", "semval", "group")


class Sched:
    ENG = ("pe", "act", "dve", "pool", "sp")
    COMPUTE = ("pe", "act", "dve", "pool")

    def __init__(self):
        self.q = {e: [] for e in self.ENG}
        self.lastw = {}
        self.readers = {}
        self.dma_count = {}
        self.dma_last = {}
        self.last_compute = {}

    def add(self, eng, fn, reads=(), writes=(), dma=None, group=None):
        op = Op()
        op.eng = eng
        op.fn = fn
        op.signal = False
        op.tick = None
        op.dma = dma
        op.group = group
        op.semval = None
        deps = {}

        def adddep(d):
            if d is op:
                return
            if group is not None and d.group == group:
                return
            if d.dma is None and d.eng == "pe" and eng == "pe" and dma is None:
                return
            deps[id(d)] = d

        if dma is not None:
            self.dma_count[dma] = self.dma_count.get(dma, 0) + 16
            op.semval = self.dma_count[dma]
            prev = self.dma_last.get(dma)
            if prev is not None:
                adddep(prev)
            self.dma_last[dma] = op
        for r in reads:
            w = self.lastw.get(r)
            if w is not None:
                adddep(w)
        for w_ in writes:
            w = self.lastw.get(w_)
            if w is not None:
                adddep(w)
            rd = self.readers.get(w_)
            if rd:
                for k, d in rd.items():
                    if k == "dma":
                        for dd in d:
                            adddep(dd)
                    else:
                        adddep(d)
        op.deps = list(deps.values())
        for r in reads:
            rd = self.readers.setdefault(r, {})
            if dma is not None:
                rd.setdefault("dma", []).append(op)
            else:
                rd[eng] = op
        for w_ in writes:
            self.lastw[w_] = op
            self.readers[w_] = {}
        self.q[eng].append(op)
        if dma is None and fn is not None:
            self.last_compute[eng] = op
        return op

    def barrier(self):
        lasts = [self.last_compute[e] for e in self.COMPUTE if e in self.last_compute]
        for e in self.ENG:
            op = Op()
            op.eng = e
            op.fn = None
            op.signal = False
            op.tick = None
            op.dma = None
            op.group = None
            op.semval = None
            op.deps = [d for d in lasts if d.eng != e]
            self.q[e].append(op)
        self.lastw = {k: v for k, v in self.lastw.items() if v.dma is not None}
        newr = {}
        for k, rd in self.readers.items():
            if "dma" in rd and rd["dma"]:
                newr[k] = {"dma": rd["dma"]}
        self.readers = newr

    def finalize(self):
        for e in self.ENG:
            for op in self.q[e]:
                for d in op.deps:
                    if d.dma is None:
                        d.signal = True
        for e in self.ENG:
            t = 0
            for op in self.q[e]:
                if op.signal:
                    t += 1
                    op.tick = t

    def emit_stream(self, e, eng, esem, dsem):
        waited = {}
        for op in self.q[e]:
            need = {}
            for d in op.deps:
                if d.dma is not None:
                    key = ("d", d.dma)
                    val = d.semval
                else:
                    key = ("e", d.eng)
                    val = d.tick
                if need.get(key, 0) < val:
                    need[key] = val
            for key, val in need.items():
                if waited.get(key, 0) < val:
                    sem = dsem[key[1]] if key[0] == "d" else esem[key[1]]
                    eng.wait_ge(sem, val)
                    waited[key] = val
            if op.fn is not None:
                ins = op.fn(eng)
                if op.dma is not None:
                    ins.then_inc(dsem[op.dma], 16)
                elif op.signal:
                    ins.then_inc(esem[e], 1)


LN_GB_ENG = "pool"


def build_program(nseq=NSEQ, stop_after=None, dbg=(), ffn_stage=4):
    nc = bass.Bass("TRN2", target_bir_lowering=False)
    sch = Sched()
    ntok = nseq * S

    def din(name, shape, dt=F32):
        return nc.dram_tensor(name, list(shape), dt, kind="ExternalInput").ap()

    x_d = din("x", [ntok, D])
    ln_g = [din("ln%d_g" % i, [1, D]) for i in (1, 2, 3)]
    ln_b = [din("ln%d_b" % i, [1, D]) for i in (1, 2, 3)]
    wg_d = [din("ffn%d_w_gate" % i, [D, FF]) for i in (1, 2)]
    wu_d = [din("ffn%d_w_up" % i, [D, FF]) for i in (1, 2)]
    wd_d = [din("ffn%d_w_down" % i, [FF, D]) for i in (1, 2)]
    win_d = din("w_in", [D, 4096])
    lam_d = [din(n, [1, 64]) for n in ("lambda_q1", "lambda_k1", "lambda_q2", "lambda_k2")]
    subg_d = din("attn_subln_g", [1, 128])
    poolw_d = din("pool_w", [4, 128, 128])
    pscale_d = din("pool_scale", [1, 512])
    wba_d = din("w_branch_att", [512, D])
    wbp_d = din("w_branch_pool", [512, D])
    wout_d = din("w_out", [D, D])
    cos_d = din("c_cos", [128, NBLK, 32])
    sin_d = din("c_sin", [128, NBLK, 32])
    identf_d = din("c_identf", [128, 128])
    identb_d = din("c_identb", [128, 128], BF16)
    ptbl_d = din("c_ptbl", [128, 4, 16])
    y_d = nc.dram_tensor("y", [ntok, D], F32, kind="ExternalOutput").ap()

    dbg_out = {}

    def dbg_tensor(name, shape, dt):
        t = nc.dram_tensor("dbg_" + name, list(shape), dt, kind="ExternalOutput").ap()
        dbg_out[name] = t
        return t

    def scr(name, shape):
        return nc.dram_tensor(name, list(shape), BF16).ap()

    wgu_s = [scr("wgu_s%d" % i, [FC // 2, 128, 2, DC, 256]) for i in (1, 2)]
    wd_s = [scr("wd_s%d" % i, [2, 128, G, 1024]) for i in (1, 2)]
    wqkv_s = scr("wqkv_s", [3, 128, DC, 512])
    wu_in_s = scr("wu_in_s", [4, 128, DC, 128])
    wgate_s = scr("wgate_s", [16, 128, DC, 128])
    wba_s = scr("wba_s", [128, 4, 1024])
    wbp_s = scr("wbp_s", [128, 4, 1024])
    wout_s = scr("wout_s", [128, 8, 1024])

    es = ExitStack()
    with es:
        def sb(name, shape, dt):
            return es.enter_context(nc.sbuf_tensor(name, list(shape), dt))

        x1buf = sb("x1buf", [128, NBLK, D], F32)
        arena = sb("arena", [128, 24704], BF16)
        arena4 = sb("arena4", [128, 8704], F32)
        ypoolT = sb("ypoolT", [128, 4, S], BF16)
        arena3 = sb("arena3", [128, 4096], BF16)
        wsl = sb("wsl", [128, 2, WSLOT], BF16)
        gb = sb("gb", [128, 2, D], F32)
        identf = sb("identf", [128, 128], F32)
        identb = sb("identb", [128, 128], BF16)
        cosT = sb("cosT", [128, NBLK, 32], F32)
        sinT = sb("sinT", [128, NBLK, 32], F32)
        g08 = sb("g08", [128, 128], F32)
        poolw = sb("poolw", [128, 4, 128], BF16)
        pscale = sb("pscale", [128, 4], F32)
        ptbl = sb("ptbl", [128, 4, 16], F32)
        lamt = sb("lamt", [128, 4, 64], F32)
        small = sb("small", [128, 128], F32)
        mv = sb("mv", [128, 8, 2], F32)
        stats = sb("stats", [128, 8, 12], F32)

        ps = [es.enter_context(nc.psum_tensor("ps%d" % i, [128, 512], F32)) for i in range(8)]

        actT = arena[:, 0:G * 1024].rearrange("p (f t) -> p f t", f=G)
        wdv = arena[:, G * 1024:2 * G * 1024].rearrange("p (f d) -> p f d", f=G)
        qT = arena[:, 0:8192].rearrange("p (h t) -> p h t", h=4)
        kT = arena[:, 8192:16384].rearrange("p (h t) -> p h t", h=4)
        vaug = arena[:, 16384:16384 + 16 * 4 * 130].rearrange("p (k h e) -> p k h e", k=16, h=4)
        wgA = arena[:, 8192:16384].rearrange("p (c k f) -> p c k f", c=8, k=DC)
        wgB = arena[:, 0:8192].rearrange("p (c k f) -> p c k f", c=8, k=DC)

        def wgate_chunk(att_pool, mc):
            w = wgA if mc < 4 else wgB
            return w[:, 4 * att_pool + (mc % 4), :, :], ("wgA" if mc < 4 else "wgB")

        wg_calls = [0]

        def load_wgate_half(hf, extra_writes=()):
            wg_calls[0] += 1
            gid = ("wg", wg_calls[0])
            w = wgA if hf == 0 else wgB
            key = "wgA" if hf == 0 else "wgB"
            sem = "cw1" if hf == 0 else "cw3"
            for ap in range(2):
                c0 = 8 * ap + 4 * hf
                dma(w[:, 4 * ap:4 * ap + 4, :, :].rearrange("p c k f -> p c (k f)"),
                    wgate_s[c0:c0 + 4].rearrange("c p k f -> p c (k f)"), sem, scr_keys("cv_c"),
                    [key] + list(extra_writes), group=gid)
        woutv = arena[:, 16384:24576].rearrange("p (m d) -> p m d", m=8)

        uT = arena4[:, 0:4 * UW].rearrange("p (g t) -> p g t", g=4)
        a4b = arena4[:, :].bitcast(BF16)
        xt16 = a4b[:, 0:8192].rearrange("p (c t) -> p c t", c=DC)
        silu_tmp = arena4[:, 4096:5120].rearrange("p (a t) -> p a t", a=2)
        yattT = a4b[:, 0:8192].rearrange("p (h t) -> p h t", h=4)
        mergedT = a4b[:, 8192:12288].rearrange("p (m t) -> p m t", m=8)
        sigtmp = arena4[:, 6144:8192].rearrange("p (a t) -> p a t", a=4)

        xt8 = arena3[:, :].rearrange("p (c t) -> p c t", c=DC)
        ptile = arena3[:, :].rearrange("p (a t) -> p a t", a=8)
        pooled = arena3[:, 0:S]

        wsl_f32 = [wsl[:, i, :].bitcast(F32) for i in range(2)]

        gbf = gb[:, :, :].rearrange("p a d -> p (a d)")
        gbb = gbf.bitcast(BF16)
        ypf = ypoolT[:, :, :].rearrange("p a t -> p (a t)")
        ropeA = ypf[:, 0:512].bitcast(F32).rearrange("p (g i) -> p g i", g=8)
        ropeB = ypf[:, 512:1024].bitcast(F32).rearrange("p (g i) -> p g i", g=8)
        qrope = [ypf[:, 1024 + 512 * i:1024 + 512 * (i + 1)] for i in range(4)]
        ocopy = gbf[:, 0:1032].rearrange("p (b c) -> p b c", b=4)
        sqbuf = gbf[:, 0:512].rearrange("p (q e) -> p q e", q=4)
        obuf = gbf[:, 1032:1544].rearrange("p (q e) -> p q e", q=4)
        yatt_tok = gbb[:, 3088:3600].rearrange("p (q e) -> p q e", q=4)

        neglam = small[:, 0:1]
        eps_ln = small[:, 1:2]
        eps_rms = small[:, 2:3]
        lsum = small[:, 4:6]
        lexp = small[:, 6:8]
        rec = small[:, 8:16]
        nr2 = small[:, 16:20]
        ssq = small[:, 20:24]
        lnr = small[:, 24:28]
        rinv = small[:, 28:32]
        lnv = small[:, 32:40]
        rstd = small[:, 40:48]
        nmr = small[:, 48:56]
        ptmp = small[:, 64:80]

        def psb(i):
            return ps[i][:, :].bitcast(BF16)

        def mm(out, lhsT, rhs, start, stop, reads, writes, skip=False):
            sch.add("pe", lambda e: e.matmul(out, lhsT, rhs, start=start, stop=stop,
                                             skip_group_check=skip), reads, writes)

        def tr(out, in_, ident, reads, writes):
            sch.add("pe", lambda e: e.transpose(out, in_, ident), reads, writes)

        def act(out, in_, func, reads, writes, bias=None, scale=None):
            kw = {}
            if bias is not None:
                kw["bias"] = bias
            if scale is not None:
                kw["scale"] = scale
            sch.add("act", lambda e: e.activation(out=out, in_=in_, func=func, **kw), reads, writes)

        def vcopy(engname, out, in_, reads, writes):
            if engname == "act":
                act(out, in_, AF.Copy, reads, writes)
            else:
                sch.add(engname, lambda e: e.tensor_copy(out, in_), reads, writes)

        def tt(out, in0, in1, op, reads, writes, eng="dve"):
            sch.add(eng, lambda e: e.tensor_tensor(out, in0, in1, op), reads, writes)

        def ts(out, in0, s1, s2, op0, op1, reads, writes, eng="dve"):
            if op1 is None:
                sch.add(eng, lambda e: e.tensor_scalar(out, in0, s1, None, op0), reads, writes)
            else:
                sch.add(eng, lambda e: e.tensor_scalar(out, in0, s1, s2, op0, op1), reads, writes)

        def stt(out, in0, scalar, in1, op0, op1, reads, writes):
            sch.add("dve", lambda e: e.scalar_tensor_tensor(out, in0, scalar, in1, op0, op1),
                    reads, writes)

        def memset(eng, ap, val, reads, writes):
            sch.add(eng, lambda e: e.memset(ap, val), reads, writes)

        def dma(out, in_, sem, reads, writes, group=None, q="sp", nonc=False):
            if nonc:
                sch.add(q, lambda e: e.dma_start(out=out, in_=in_, allow_slow_non_contiguous=True),
                        reads, writes, dma=sem, group=group)
            else:
                sch.add(q, lambda e: e.dma_start(out=out, in_=in_), reads, writes, dma=sem, group=group)

        conv_cnt = {}
        conv_batches = []

        def conv(out, in_, grp):
            n = conv_cnt.get(grp, 0)
            conv_cnt[grp] = n + 1
            key = (grp, n // CONV_BATCH)
            if n % CONV_BATCH == 0:
                if len(conv_batches) >= 1:
                    sch.add("pool", None, (), ())
                    sch.q["pool"][-1].deps = [conv_batches[-1][1]]
                conv_batches.append([key, None])
            sch.add("pool", lambda e: e.dma_start(out=out, in_=in_), (), [("scr",) + key], dma=key, group=key)
            conv_batches[-1][1] = sch.q["pool"][-1]

        def scr_keys(grp):
            return [("scr", grp, k) for k in range((conv_cnt[grp] + CONV_BATCH - 1) // CONV_BATCH)]

        def conv_ffn(i, half):
            wg_v = wg_d[i].rearrange("(c p) (n f) -> n p c f", p=128, f=256)
            wu_v = wu_d[i].rearrange("(c p) (n f) -> n p c f", p=128, f=256)
            pairs = range(0, 6) if half == 0 else range(6, 11)
            for pp in pairs:
                grp = "cv_f%d_q%d" % (i + 1, pp // 2)
                conv(wgu_s[i][pp, :, 0, :, :], wg_v[pp], grp)
                conv(wgu_s[i][pp, :, 1, :, :], wu_v[pp], grp)
            grp = "cv_f%d_d%d" % (i + 1, half)
            for j in range(G):
                fc = half * G + j
                conv(wd_s[i][half, :, j, :], wd_d[i][fc * 128:(fc + 1) * 128, :], grp)

        dma(identf[:, :], identf_d, "c0", (), ["identf"])
        dma(identb[:, :], identb_d, "c1", (), ["identb"])
        dma(cosT[:, :, :], cos_d, "c2", (), ["cos"])
        dma(sinT[:, :, :], sin_d, "c3", (), ["sin"])
        dma(ptbl[:, :, :], ptbl_d, "c4", (), ["ptbl"])
        for i in range(4):
            dma(lamt[:, i, :], lam_d[i].broadcast_to([128, 64]), "c5", (), ["lamt"], group="lamt")
        dma(g08[:, :], subg_d.broadcast_to([128, 128]), "c6", (), ["g08"])
        dma(pscale[:, :], pscale_d.rearrange("o (g d) -> d (o g)", g=4), "c7", (), ["pscale"], nonc=True)

        conv_ffn(0, 0)
        dma(poolw[:, :, :], poolw_d.rearrange("g c d -> c g d"), "c8", (), ["poolw"], q="pool")
        conv_ffn(0, 1)
        for g3 in range(3):
            conv(wqkv_s[g3], win_d[:, g3 * 512:(g3 + 1) * 512].rearrange("(c p) n -> p c n", p=128), "cv_in")
        uview = win_d[:, 1536:2048].rearrange("(c p) (n f) -> n p c f", p=128, f=128)
        for ch in range(4):
            conv(wu_in_s[ch], uview[ch], "cv_in")
        gview = win_d[:, 2048:4096].rearrange("(c p) (n f) -> n p c f", p=128, f=128)
        for ch in range(16):
            conv(wgate_s[ch], gview[ch], "cv_c")
        conv(wba_s, wba_d.rearrange("(k p) m -> p k m", p=128), "cv_c")
        conv(wbp_s, wbp_d.rearrange("(k p) m -> p k m", p=128), "cv_c")
        conv(wout_s, wout_d.rearrange("(k p) m -> p k m", p=128), "cv_c")
        conv_ffn(1, 0)
        conv_ffn(1, 1)

        memset("dve", eps_ln, EPS_LN, (), ["eps"])
        memset("dve", eps_rms, EPS_RMS, (), ["eps"])
        tt(lamt[:, 0, :], lamt[:, 0, :], lamt[:, 1, :], ALU.mult, ["lamt"], ["lamt"])
        tt(lamt[:, 2, :], lamt[:, 2, :], lamt[:, 3, :], ALU.mult, ["lamt"], ["lamt"])
        sch.add("dve", lambda e: e.tensor_reduce(lsum[:, 0:1], lamt[:, 0, :], AX.X, ALU.add), ["lamt"], ["lsum"])
        sch.add("dve", lambda e: e.tensor_reduce(lsum[:, 1:2], lamt[:, 2, :], AX.X, ALU.add), ["lamt"], ["lsum"])
        act(lexp, lsum, AF.Exp, ["lsum"], ["lexp"])
        tt(neglam, lexp[:, 1:2], lexp[:, 0:1], ALU.subtract, ["lexp"], ["neglam"])
        ts(neglam, neglam, -LAMBDA_INIT, None, ALU.add, None, ["neglam"], ["neglam"])
        ts(g08[:, :], g08[:, :], 1.0 - LAMBDA_INIT, None, ALU.mult, None, ["g08"], ["g08"])

        tcount = [0]

        def transposes_f32(bi, dst, dstkey, col0):
            b0 = (tcount[0] % 2) * 2
            tcount[0] += 1
            for dc in range(DC):
                bank = b0 + dc // 4
                tr(ps[bank][:, (dc % 4) * 128:(dc % 4 + 1) * 128], x1buf[:, bi, dc * 128:(dc + 1) * 128],
                   identf[:, :], [("x1", bi), "identf"], [("ps", bank)])
            for hb in range(2):
                engn = "dve" if hb == 0 else "act"
                vcopy(engn, dst[:, hb * 4:(hb + 1) * 4, col0:col0 + 128],
                      ps[b0 + hb][:, :].rearrange("p (c t) -> p c t", c=4),
                      [("ps", b0 + hb)], [(dstkey, col0 // 128, hb)])

        def load_gb(idx):
            dma(gb[:, 0, :], ln_g[idx].broadcast_to([128, D]), "gb", (), ["gb"], group=("gb", idx, tcount[0]))
            dma(gb[:, 1, :], ln_b[idx].broadcast_to([128, D]), "gb", (), ["gb"], group=("gb", idx, tcount[0]))

        def layernorm_gen(blocks, after=None, exposed=False, after2=None):
            n = len(blocks)
            for j, bi in enumerate(blocks):
                sch.add("dve", lambda e, j=j, bi=bi: e.bn_stats(stats[:, j, 0:6], x1buf[:, bi, 0:512]),
                        [("x1", bi)], [("st", j)])
                sch.add("dve", lambda e, j=j, bi=bi: e.bn_stats(stats[:, j, 6:12], x1buf[:, bi, 512:1024]),
                        [("x1", bi)], [("st", j)])
                sch.add("dve", lambda e, j=j: e.bn_aggr(mv[:, j, :], stats[:, j, :]), [("st", j)], [("mv", j)])
                act(lnv[:, j:j + 1], mv[:, j, 1:2], AF.Ln, [("mv", j), "eps"], [("lnv", j)], bias=eps_ln, scale=1.0)
                act(rstd[:, j:j + 1], lnv[:, j:j + 1], AF.Exp, [("lnv", j)], [("rstd", j)], scale=-0.5)
                stt(nmr[:, j:j + 1], mv[:, j, 0:1], -1.0, rstd[:, j:j + 1], ALU.mult, ALU.mult,
                    [("mv", j), ("rstd", j)], [("nmr", j)])
                yield
            for j, bi in enumerate(blocks):
                act(x1buf[:, bi, :], x1buf[:, bi, :], AF.Identity, [("x1", bi), ("rstd", j), ("nmr", j)],
                    [("x1", bi)], bias=nmr[:, j:j + 1], scale=rstd[:, j:j + 1])
                if exposed:
                    engn = "pool" if (j % 8) in (1, 3, 6) else "dve"
                else:
                    engn = "pool"
                tt(x1buf[:, bi, :], x1buf[:, bi, :], gb[:, 0, :], ALU.mult, [("x1", bi), "gb"], [("x1", bi)],
                   eng=engn)
                tt(x1buf[:, bi, :], x1buf[:, bi, :], gb[:, 1, :], ALU.add, [("x1", bi), "gb"], [("x1", bi)],
                   eng=engn)
                yield
            if after is not None:
                after()
            if after2 is not None:
                for _ in range(5):
                    yield
                after2()

        def drain(gen, k=None):
            if gen is None:
                return None
            try:
                if k is None:
                    while True:
                        next(gen)
                else:
                    for _ in range(k):
                        next(gen)
            except StopIteration:
                return None
            return gen

        wpair = [0]

        def ffn_load_pair(which, pp):
            si = wpair[0] % 2
            wpair[0] += 1
            rds = scr_keys("cv_f%d_q%d" % (which + 1, pp // 2))
            dma(wsl[:, si, 0:4096], wgu_s[which][pp].rearrange("p g c f -> p (g c f)"),
                ("ws", si), rds, [("ws", si)])
            return si

        def load_x(seq, T):
            r0 = seq * S + T * 1024
            for hf in range(2):
                b0_ = T * 8 + 4 * hf
                dma(x1buf[:, b0_:b0_ + 4, :],
                    x_d[r0 + 512 * hf:r0 + 512 * (hf + 1), :].rearrange("(j p) d -> p j d", p=128),
                    "xin%d" % hf, (), [("x1", b0_ + j) for j in range(4)])

        def ffn_tile(seq, T, which, pending=None, after_ln=None, exposed=False, early=None):
            blocks = [T * 8 + j for j in range(8)]
            r0 = seq * S + T * 1024
            slots = {}
            slots[0] = ffn_load_pair(which, 0)
            slots[1] = ffn_load_pair(which, 1)

            for j, bi in enumerate(blocks):
                transposes_f32(bi, xt16, "xt16", j * 128)
            par = 0
            if ffn_stage < 2:
                return
            for gi in range(2):
                grp = "cv_f%d_d%d" % (which + 1, gi)
                dma(wdv, wd_s[which][gi], "wd", scr_keys(grp), ["wd"])
                for fl in range(G):
                    fc = gi * G + fl
                    pp, a = fc // 2, fc % 2
                    si = slots[pp]
                    wv = wsl[:, si, 0:4096].rearrange("p (g c a f) -> p a g c f", g=2, c=DC, a=2)
                    for half in range(2):
                        gbank, ubank = 4 + 2 * par, 5 + 2 * par
                        xk = [("xt16", half * 4 + jb, hb) for jb in range(4) for hb in range(2)]
                        for gu, bank in ((0, gbank), (1, ubank)):
                            for dc in range(DC):
                                mm(ps[bank][:, :], wv[:, a, gu, dc, :], xt16[:, dc, half * 512:(half + 1) * 512],
                                   dc == 0, dc == DC - 1, [("ws", si)] + xk, [("ps", bank)])
                        act(silu_tmp[:, par, :], ps[gbank][:, :], AF.Silu, [("ps", gbank)], [("silu", par)])
                        tt(actT[:, fl, half * 512:(half + 1) * 512], ps[ubank][:, :], silu_tmp[:, par, :], ALU.mult,
                           [("ps", ubank), ("silu", par)], [("act", fl, half)])
                        par ^= 1
                    if a == 1 and pp + 2 <= (FC // 2) - 1:
                        slots[pp + 2] = ffn_load_pair(which, pp + 2)
                    pending = drain(pending, 1)
                for r in range(4 if ffn_stage >= 3 else 0):
                    for jj in range(2):
                        j = 2 * r + jj
                        bi = blocks[j]
                        for dh in range(2):
                            bank = (r % 2) * 4 + jj * 2 + dh
                            for fl in range(G):
                                mm(ps[bank][:, :], actT[:, fl, j * 128:(j + 1) * 128],
                                   wdv[:, fl, dh * 512:(dh + 1) * 512], fl == 0, fl == G - 1,
                                   [("act", fl, j // 4), "wd"], [("ps", bank)])
                            stt(x1buf[:, bi, dh * 512:(dh + 1) * 512], ps[bank][:, :], 0.5 * INV_ALPHA,
                                x1buf[:, bi, dh * 512:(dh + 1) * 512], ALU.mult, ALU.add,
                                [("ps", bank), ("x1", bi)], [("x1", bi)])
            pending = drain(pending)
            if T == 0:
                load_gb(0 if which == 0 else 2)

            def after():
                if which == 1:
                    for hf in range(2):
                        b0_ = T * 8 + 4 * hf
                        dma(y_d[r0 + 512 * hf:r0 + 512 * (hf + 1), :].rearrange("(j p) d -> p j d", p=128),
                            x1buf[:, b0_:b0_ + 4, :], "yout%d" % hf, [("x1", b0_ + j) for j in range(4)],
                            [("yout", hf)])
            return layernorm_gen(blocks, after, exposed, after_ln)

        def phase_a2(seq, pending=None):
            memset("dve", vaug[:, :, :, 128:130], 1.0, (), ["vones"])
            memset("dve", uT[:, :, 0:UOFF], 0.0, (), ["upad"])
            memset("dve", uT[:, :, UOFF + S:UW], 0.0, (), ["upad"])
            ropepar = [0]
            pend_a2 = [pending]
            for tq in range(4):
                if tq == 2:
                    pend_a2[0] = drain(pend_a2[0])
                for jb in range(4):
                    transposes_f32(tq * 4 + jb, xt8, "xt8", jb * 128)
                xk_all = [("xt8", jb, hb) for jb in range(4) for hb in range(2)]
                wvs = []
                for qk in range(2):
                    dma(wsl[:, qk, 0:4096], wqkv_s[qk].rearrange("p c n -> p (c n)"), ("ws", qk),
                        scr_keys("cv_in"), [("ws", qk)])
                    wvs.append(wsl[:, qk, 0:4096].rearrange("p (c n) -> p c n", c=DC))
                rps = {}

                def qk_proj(qk, jb):
                    bi = tq * 4 + jb
                    bank = 4 + 2 * qk + (jb % 2)
                    for dc in range(DC):
                        mm(ps[bank][:, :], xt8[:, dc, jb * 128:(jb + 1) * 128], wvs[qk][:, dc, :], dc == 0,
                           dc == DC - 1, [("ws", qk), ("xt8", jb, 0), ("xt8", jb, 1)], [("ps", bank)])
                    rp = ropepar[0] % 4
                    ropepar[0] += 1
                    rps[(qk, jb)] = rp
                    src = ps[bank][:, :].rearrange("p (g t i) -> p g t i", g=8, t=2)
                    dst = qrope[rp].rearrange("p (g t i) -> p g t i", g=8, t=2)
                    cb = cosT[:, bi, :].unsqueeze(1).to_broadcast([128, 8, 32])
                    sbb = sinT[:, bi, :].unsqueeze(1).to_broadcast([128, 8, 32])
                    pk = [("ps", bank)]
                    tt(ropeA, src[:, :, 0, :], cb, ALU.mult, pk + ["cos"], ["ropeA"])
                    tt(ropeB, src[:, :, 1, :], sbb, ALU.mult, pk + ["sin"], ["ropeB"])
                    tt(dst[:, :, 0, :], ropeA, ropeB, ALU.subtract, ["ropeA", "ropeB"], [("qrope", rp)])
                    tt(ropeA, src[:, :, 0, :], sbb, ALU.mult, pk + ["sin"], ["ropeA"])
                    tt(ropeB, src[:, :, 1, :], cb, ALU.mult, pk + ["cos"], ["ropeB"])
                    tt(dst[:, :, 1, :], ropeA, ropeB, ALU.add, ["ropeA", "ropeB"], [("qrope", rp)])

                def qk_tr(qk, jb):
                    rp = rps[(qk, jb)]
                    tb0 = 0 if qk == 0 else 2
                    for h in range(4):
                        tbank = tb0 + h // 2
                        c0 = (h % 2) * 512 + jb * 128
                        tr(psb(tbank)[:, c0:c0 + 128], qrope[rp][:, h * 128:(h + 1) * 128], identb[:, :],
                           [("qrope", rp), "identb"], [("ps", tbank)])

                def qk_evac(qk):
                    dstT = qT if qk == 0 else kT
                    tb0 = 0 if qk == 0 else 2
                    for h in range(4):
                        tbank = tb0 + h // 2
                        vcopy("dve" if h // 2 == 0 else "act", dstT[:, h, tq * 512:(tq + 1) * 512],
                              psb(tbank)[:, (h % 2) * 512:(h % 2 + 1) * 512], [("ps", tbank)], [("qkT", qk, h)])

                qk_proj(0, 0)
                qk_proj(1, 0)
                for jb in range(1, 4):
                    qk_proj(0, jb)
                    qk_proj(1, jb)
                    qk_tr(0, jb - 1)
                    qk_tr(1, jb - 1)
                tail_qk = [lambda: (qk_tr(0, 3), qk_tr(1, 3), qk_evac(0), qk_evac(1))]
                si = 0
                dma(wsl[:, si, 0:4096], wqkv_s[2].rearrange("p c n -> p (c n)"), ("ws", si),
                    scr_keys("cv_in"), [("ws", si)])
                wv = wsl[:, si, 0:4096].rearrange("p (c n) -> p c n", c=DC)
                for jb in range(4):
                    pend_a2[0] = drain(pend_a2[0], 1)
                    bi = tq * 4 + jb
                    bank = 4 + (jb % 2)
                    for dc in range(DC):
                        mm(ps[bank][:, :], xt8[:, dc, jb * 128:(jb + 1) * 128], wv[:, dc, :], dc == 0, dc == DC - 1,
                           [("ws", si), ("xt8", jb, 0), ("xt8", jb, 1)], [("ps", bank)])
                    vcopy("act" if jb % 2 == 0 else "dve", vaug[:, bi, :, 0:128],
                          ps[bank][:, :].rearrange("p (h e) -> p h e", h=4), [("ps", bank)], [("v", bi)])
                si = 1
                dma(wsl[:, si, 0:4096].rearrange("p (c x) -> p c x", c=4), wu_in_s.rearrange("c p k f -> p c (k f)"),
                    ("ws", si), scr_keys("cv_in"), [("ws", si)])
                wv = wsl[:, si, 0:4096].rearrange("p (c k f) -> p c k f", c=4, k=DC)
                for ch in range(4):
                    pend_a2[0] = drain(pend_a2[0], 1)
                    bank = 6 + (ch % 2)
                    for dc in range(DC):
                        mm(ps[bank][:, :], wv[:, ch, dc, :], xt8[:, dc, :], dc == 0, dc == DC - 1,
                           [("ws", si)] + xk_all, [("ps", bank)])
                    vcopy("dve" if ch % 2 == 0 else "act", uT[:, ch, UOFF + tq * 512:UOFF + (tq + 1) * 512],
                          ps[bank][:, :], [("ps", bank), "upad"], [("u", ch)])
                    if ch == 0:
                        tail_qk[0]()
            Ta, Tb = wsl_f32[0], wsl_f32[1]
            xk_all = [("xt8", jb, hb) for jb in range(4) for hb in range(2)]
            for g4 in range(4):
                k = g4 + 1
                w = 1 << k
                U = uT[:, g4, :]
                bufs = [Ta, Tb]
                keys = [("ws", 0), ("ws", 1)]
                srcb, srck = U, ("u", g4)
                for lv in range(k):
                    sh = 1 << lv
                    lo = 2 * sh - 1
                    dstb, dstk = bufs[lv % 2], keys[lv % 2]
                    tt(dstb[:, lo:UW], srcb[:, lo:UW], srcb[:, lo - sh:UW - sh], ALU.add, [srck, "upad"], [dstk])
                    srcb, srck = dstb, dstk
                off = UOFF + w // 2 - 1
                stt(pooled[:, 0:S], srcb[:, off:off + S], 1.0 / w, U[:, UOFF:UOFF + S], ALU.mult, ALU.subtract,
                    [srck, ("u", g4)], xk_all + ["pooled"])
                nl = w // 2
                tt(ptmp[:, 0:nl], srcb[:, off:off + nl], ptbl[:, g4, 0:nl], ALU.mult, [srck, "ptbl"], ["ptmp"])
                tt(pooled[:, 0:nl], ptmp[:, 0:nl], U[:, UOFF:UOFF + nl], ALU.subtract, ["ptmp", ("u", g4)], ["pooled"])
                nr = w // 2 - 1
                if nr > 0:
                    tt(ptmp[:, 0:nr], srcb[:, off + S - nr:off + S], ptbl[:, g4, 8:8 + nr], ALU.mult,
                       [srck, "ptbl"], ["ptmp"])
                    tt(pooled[:, S - nr:S], ptmp[:, 0:nr], U[:, UOFF + S - nr:UOFF + S], ALU.subtract,
                       ["ptmp", ("u", g4)], ["pooled"])
                for tq in range(4):
                    bank = 4 + (tq % 2)
                    mm(ps[bank][:, :], poolw[:, g4, :], pooled[:, tq * 512:(tq + 1) * 512], True, True,
                       ["pooled", "poolw"], [("ps", bank)])
                    act(ypoolT[:, g4, tq * 512:(tq + 1) * 512], ps[bank][:, :], AF.Identity, [("ps", bank), "pscale"],
                        [("ypool", g4), "ropeA", "ropeB"] + [("qrope", i) for i in range(4)],
                        scale=pscale[:, g4:g4 + 1])

        def phase_b(seq):
            sbank = [0]
            pt = [0]
            pending = [None]

            def emit_S(h, qt, kc, m):
                sb_ = 4 + (sbank[0] % 3)
                sbank[0] += 1
                pi = pt[0] % 8
                pt[0] += 1
                kz = wsl[:, h % 2, 0:4096].rearrange("p (m t) -> p m t", m=2)
                mm(ps[sb_][:, :], kz[:, m, kc * 128:(kc + 1) * 128],
                   qT[:, h, qt * 512:(qt + 1) * 512], True, True,
                   [("qkT", 0, h), ("kz", h % 2)], [("ps", sb_)])
                act(ptile[:, pi, :], ps[sb_][:, :], AF.Exp, [("ps", sb_)], [("pt", pi)], scale=0.125)
                return pi

            def emit_PV(h, kc, m, pi):
                for qb in range(4):
                    ob = 2 * m + qb // 2
                    c0 = (qb % 2) * 129
                    mm(ps[ob][:, c0:c0 + 129], ptile[:, pi, qb * 128:(qb + 1) * 128],
                       vaug[:, kc, h, 0:129], (kc == 0 and qb % 2 == 0), kc == 15,
                       [("pt", pi), ("v", kc), "vones"], [("ps", ob)], skip=True)

            def emit_post(h, qt):
                for ob in range(4):
                    vcopy("dve", ocopy[:, ob, :], ps[ob][:, 0:258], [("ps", ob)], [("oc", ob)])
                ocf = gbf[:, 0:1032]
                allo = [("oc", ob) for ob in range(4)]
                sch.add("dve", lambda e: e.reciprocal(rec.rearrange("p (b j) -> p b j", b=4),
                                                      ocopy[:, :, 128:258:129]), allo, ["rec"])
                ts(nr2, rec[:, 4:8], neglam, None, ALU.mult, None, ["rec", "neglam"], ["nr2"])
                o1v = ocf[:, 0:516].rearrange("p (q e) -> p q e", q=4)[:, :, 0:128]
                o2v = ocf[:, 516:1032].rearrange("p (q e) -> p q e", q=4)[:, :, 0:128]
                tt(obuf, o1v, rec[:, 0:4].unsqueeze(2).to_broadcast([128, 4, 128]), ALU.mult, allo + ["rec"],
                   [("obuf", qb) for qb in range(4)])
                tt(o2v, o2v, nr2.unsqueeze(2).to_broadcast([128, 4, 128]), ALU.mult, allo + ["nr2"], allo)
                tt(obuf, obuf, o2v, ALU.add, allo + [("obuf", qb) for qb in range(4)],
                   [("obuf", qb) for qb in range(4)])
                ok = [("obuf", qb) for qb in range(4)]
                tt(sqbuf, obuf, obuf, ALU.mult, ok, ["sq"] + [("oc", ob) for ob in range(4)])
                sch.add("dve", lambda e: e.tensor_reduce(ssq, sqbuf, AX.X, ALU.add),
                        ["sq"] + [("oc", ob) for ob in range(4)], ["ssq"])
                def fin2():
                    act(lnr, ssq, AF.Ln, ["ssq", "eps"], ["lnr"], bias=eps_rms, scale=1.0 / 128.0)
                    act(rinv, lnr, AF.Exp, ["lnr"], ["rinv"], scale=-0.5)
                    for qb in range(4):
                        stt(yatt_tok[:, qb, :], obuf[:, qb, :], rinv[:, qb:qb + 1], g08[:, :], ALU.mult, ALU.mult,
                            [("obuf", qb), "rinv", "g08"], [("yat", qb)])

                def fin():
                    for qb in range(4):
                        tr(psb(7)[:, qb * 128:(qb + 1) * 128], yatt_tok[:, qb, :], identb[:, :],
                           [("yat", qb), "identb"], [("ps", 7)])
                    vcopy("dve", yattT[:, h, qt * 512:(qt + 1) * 512], psb(7)[:, 0:512], [("ps", 7)],
                          [("yattT", h)])
                return fin2, fin

            for sl in range(2):
                kzs = wsl[:, sl, 0:4096].rearrange("p (m t) -> p m t", m=2)
                memset("dve", kzs[64:128, 0, :], 0.0, (), [("kz", sl)])
                memset("dve", kzs[0:64, 1, :], 0.0, (), [("kz", sl)])
            def kz_copies(h):
                kzv = wsl[:, h % 2, 0:4096].rearrange("p (m t) -> p m t", m=2)
                sch.add("dve", lambda e: e.tensor_copy(kzv[0:64, 0, :], kT[0:64, h, :]),
                        [("qkT", 1, h)], [("kz", h % 2)])
                sch.add("dve", lambda e: e.tensor_copy(kzv[64:128, 1, :], kT[64:128, h, :]),
                        [("qkT", 1, h)], [("kz", h % 2)])
                if h == 3:
                    load_wgate_half(0, [("qkT", 1, hh) for hh in range(4)])

            kz_copies(0)
            kz_copies(1)
            for h in range(4):
                for qt in range(4):
                    steps = [(kc, m) for kc in range(16) for m in range(2)]
                    pis = {}
                    for i in range(2):
                        pis[i] = emit_S(h, qt, steps[i][0], steps[i][1])
                    for i in range(32):
                        if i + 2 < 32:
                            pis[i + 2] = emit_S(h, qt, steps[i + 2][0], steps[i + 2][1])
                        emit_PV(h, steps[i][0], steps[i][1], pis[i])
                        if i == 4 and qt == 1 and 1 <= h <= 2:
                            kz_copies(h + 1)
                        if i == 4 and qt == 1 and h == 3:
                            dma(wsl[:, 0, 0:4096].rearrange("p (k m) -> p k m", k=4), wba_s, ("ws", 0),
                                scr_keys("cv_c"), [("ws", 0), ("kz", 0)])
                        if i == 14 and pending[0] is not None:
                            pending[0][0]()
                        if i == 24 and pending[0] is not None:
                            pending[0][1]()
                            pending[0] = None
                    pending[0] = emit_post(h, qt)
            if pending[0] is not None:
                pending[0][0]()
                pending[0][1]()
                pending[0] = None

        def phase_c(seq):
            load_wgate_half(1)
            dma(woutv, wout_s, "cw2", scr_keys("cv_c"), ["wout"])
            dma(wsl[:, 1, 0:4096].rearrange("p (k m) -> p k m", k=4), wbp_s, ("ws", 1), scr_keys("cv_c"), [("ws", 1)])
            load_gb(1)
            wba = wsl[:, 0, 0:4096].rearrange("p (k m) -> p k m", k=4)
            wbp = wsl[:, 1, 0:4096].rearrange("p (k m) -> p k m", k=4)
            par = 0
            pend_c = [None]
            for tq in range(4):
                if tq == 0:
                    for jb in range(4):
                        transposes_f32(jb, xt8, "xt8", jb * 128)
                xk_all = [("xt8", jb, hb) for jb in range(4) for hb in range(2)]
                tsl = slice(tq * 512, (tq + 1) * 512)
                import os
                cst = int(os.environ.get("C_STAGE", "9"))
                for mc in range(8 if cst >= 2 else 0):
                    b_a, b_p, b_ga, b_gp = 4 * par, 4 * par + 1, 4 * par + 2, 4 * par + 3
                    for kc in range(4):
                        mm(ps[b_a][:, :], wba[:, kc, mc * 128:(mc + 1) * 128], yattT[:, kc, tsl], kc == 0, kc == 3,
                           [("ws", 0), ("yattT", kc)], [("ps", b_a)])
                    for kc in range(4):
                        mm(ps[b_p][:, :], wbp[:, kc, mc * 128:(mc + 1) * 128], ypoolT[:, kc, tsl], kc == 0, kc == 3,
                           [("ws", 1), ("ypool", kc)], [("ps", b_p)])
                    wga_, wk = wgate_chunk(0, mc)
                    wgp_, _ = wgate_chunk(1, mc)
                    for dc in range(DC):
                        mm(ps[b_ga][:, :], wga_[:, dc, :], xt8[:, dc, :], dc == 0, dc == DC - 1,
                           [wk] + xk_all, [("ps", b_ga)])
                    for dc in range(DC):
                        mm(ps[b_gp][:, :], wgp_[:, dc, :], xt8[:, dc, :], dc == 0, dc == DC - 1,
                           [wk] + xk_all, [("ps", b_gp)])
                    sa, sp_ = sigtmp[:, 2 * par, :], sigtmp[:, 2 * par + 1, :]
                    act(sa, ps[b_ga][:, :], AF.Sigmoid, [("ps", b_ga)], [("sig", 2 * par)])
                    act(sp_, ps[b_gp][:, :], AF.Sigmoid, [("ps", b_gp)], [("sig", 2 * par + 1)])
                    tt(sa, ps[b_a][:, :], sa, ALU.mult, [("ps", b_a), ("sig", 2 * par)], [("sig", 2 * par)])
                    tt(sp_, ps[b_p][:, :], sp_, ALU.mult, [("ps", b_p), ("sig", 2 * par + 1)], [("sig", 2 * par + 1)])
                    tt(mergedT[:, mc, :], sa, sp_, ALU.add, [("sig", 2 * par), ("sig", 2 * par + 1)], [("mrg", mc)])
                    par ^= 1
                    pend_c[0] = drain(pend_c[0], 2)
                if tq < 3:
                    for jb in range(4):
                        transposes_f32((tq + 1) * 4 + jb, xt8, "xt8", jb * 128)
                blocks = [tq * 4 + jb for jb in range(4)]
                for jb, bi in enumerate(blocks if cst >= 3 else []):
                    for dh in range(2):
                        bank = (jb % 2) * 2 + dh + 4 * par
                        for mc in range(8):
                            mm(ps[bank][:, :], mergedT[:, mc, jb * 128:(jb + 1) * 128],
                               woutv[:, mc, dh * 512:(dh + 1) * 512], mc == 0, mc == 7,
                               [("mrg", mc), "wout"], [("ps", bank)])
                        stt(x1buf[:, bi, dh * 512:(dh + 1) * 512], ps[bank][:, :], INV_ALPHA,
                            x1buf[:, bi, dh * 512:(dh + 1) * 512], ALU.mult, ALU.add,
                            [("ps", bank), ("x1", bi)], [("x1", bi)])
                pend_c[0] = drain(pend_c[0])
                pend_c[0] = layernorm_gen(blocks, None, False)
            return pend_c[0]

        def dump(name, ap, shape, dt):
            t = dbg_tensor(name, shape, dt)
            sch.barrier()
            dma(t, ap, "dbg_" + name, (), ["dbgout"])

        done = False
        if stop_after == "p0":
            dump("g08", g08[:, :], [128, 128], F32)
            dma(y_d[0:128, :], x_d[0:128, :], "yout", (), ["yout"])
        else:
            load_x(0, 0)
            load_x(0, 1)
        carry = None
        for seq in range(nseq if stop_after != "p0" else 0):
            g0 = ffn_tile(seq, 0, 0, pending=carry)
            carry = None
            g1 = ffn_tile(seq, 1, 0, pending=g0)
            if stop_after == "a1" or (seq == 0 and "x1" in dbg):
                g1 = drain(g1)
            if seq == 0 and "x1" in dbg:
                dump("x1", x1buf[:, :, :], [128, NBLK, D], F32)
            if stop_after == "a1":
                break
            sch.barrier()
            phase_a2(seq, pending=g1)
            if seq == 0 and "a2" in dbg:
                dump("qT", qT, [128, 4, S], BF16)
                dump("kT", kT, [128, 4, S], BF16)
                dump("vaug", vaug, [128, 16, 4, 130], BF16)
                dump("ypoolT", ypoolT[:, :, :], [128, 4, S], BF16)
            if stop_after == "a2":
                break
            sch.barrier()
            phase_b(seq)
            if seq == 0 and "b" in dbg:
                dump("yattT", yattT, [128, 4, S], BF16)
            if stop_after == "b":
                break
            sch.barrier()
            gc = phase_c(seq)
            if stop_after == "c" or (seq == 0 and "c" in dbg):
                gc = drain(gc)
            if seq == 0 and "c" in dbg:
                dump("x2", x1buf[:, :, :], [128, NBLK, D], F32)
            if stop_after == "c":
                break
            sch.barrier()
            nxt0 = (lambda sq=seq: load_x(sq + 1, 0)) if seq + 1 < nseq else None
            nxt1 = (lambda sq=seq: load_x(sq + 1, 1)) if seq + 1 < nseq else None
            g0 = ffn_tile(seq, 0, 1, pending=gc, after_ln=nxt0)
            carry = ffn_tile(seq, 1, 1, pending=g0, after_ln=nxt1)
        drain(carry)

        sch.barrier()
        fin = [op for op in sch.dma_last.values()]
        op = Op()
        op.eng, op.fn, op.signal, op.tick, op.dma, op.group, op.semval = "sp", None, False, None, None, None, None
        op.deps = fin
        sch.q["sp"].append(op)

        sch.finalize()

        esem = {e: es.enter_context(nc.semaphore("sem_" + e)) for e in Sched.COMPUTE}
        dsem = {}
        for i, k in enumerate(sch.dma_count.keys()):
            dsem[k] = es.enter_context(nc.semaphore("dsem%d" % i))
        block = es.enter_context(nc.Block())

        @block.tensor
        def _(eng):
            sch.emit_stream("pe", eng, esem, dsem)

        @block.scalar
        def _(eng):
            sch.emit_stream("act", eng, esem, dsem)

        @block.vector
        def _(eng):
            sch.emit_stream("dve", eng, esem, dsem)

        @block.gpsimd
        def _(eng):
            sch.emit_stream("pool", eng, esem, dsem)

        @block.sync
        def _(eng):
            sch.emit_stream("sp", eng, esem, dsem)

    return nc, dbg_out


def _constants():
    import ml_dtypes
    inv = 1.0 / (10000.0 ** (np.arange(0, 64, 2, dtype=np.float64) / 64.0))
    pos = np.arange(S, dtype=np.float64)
    ang = (pos[:, None].astype(np.float32) * inv[None, :].astype(np.float32)).astype(np.float32)
    cos = np.cos(ang.astype(np.float64)).astype(np.float32)
    sin = np.sin(ang.astype(np.float64)).astype(np.float32)
    cosT = np.ascontiguousarray(cos.reshape(NBLK, 128, 32).transpose(1, 0, 2))
    sinT = np.ascontiguousarray(sin.reshape(NBLK, 128, 32).transpose(1, 0, 2))
    identf = np.eye(128, dtype=np.float32)
    identb = np.eye(128, dtype=np.float32).astype(ml_dtypes.bfloat16)
    tbl = np.zeros((4, 16), dtype=np.float32)
    for g in range(4):
        w = 2 << g
        for t in range(w // 2):
            tbl[g, t] = 1.0 / (t + w // 2)
        nr = w // 2 - 1
        for i in range(nr):
            t = S - nr + i
            tbl[g, 8 + i] = 1.0 / (S - t + w // 2)
    ptbl = np.ascontiguousarray(np.broadcast_to(tbl[None], (128, 4, 16))).astype(np.float32)
    return {"c_cos": cosT, "c_sin": sinT, "c_identf": identf, "c_identb": identb, "c_ptbl": ptbl}


_WNAMES = ["ln1_g", "ln1_b", "ffn1_w_gate", "ffn1_w_up", "ffn1_w_down", "w_in", "lambda_q1", "lambda_k1",
           "lambda_q2", "lambda_k2", "attn_subln_g", "pool_w", "pool_scale", "w_branch_att", "w_branch_pool",
           "w_out", "ln2_g", "ln2_b", "ffn2_w_gate", "ffn2_w_up", "ffn2_w_down", "ln3_g", "ln3_b"]


def _prep_weights(inputs):
    out = {}
    for n in _WNAMES:
        a = np.asarray(inputs[n], dtype=np.float32)
        out[n] = np.ascontiguousarray(a[0])
    out.update(_constants())
    return out


def kernel(**inputs):
    x = np.asarray(inputs["x"], dtype=np.float32)
    B = x.shape[0]
    per = B // NCORES
    wmap = _prep_weights(inputs)
    nc, _ = build_program(nseq=per)
    in_maps = []
    for c in range(NCORES):
        m = dict(wmap)
        m["x"] = np.ascontiguousarray(x[c * per:(c + 1) * per].reshape(per * S, D))
        in_maps.append(m)
    res = run_bass_kernel_spmd(nc, in_maps, core_ids=list(range(NCORES)))
    outs = [np.asarray(r["y"], dtype=np.float32).reshape(per, S, D) for r in res.results]
    return np.concatenate(outs, axis=0)
```

```python
import math
from contextlib import ExitStack

import numpy as np
import concourse.bass as bass
import concourse.mybir as mybir
from concourse.bass_utils import run_bass_kernel_spmd

F32 = mybir.dt.float32
BF16 = mybir.dt.bfloat16
AF = mybir.ActivationFunctionType
ALU = mybir.AluOpType
AX = mybir.AxisListType

NCORES = 8
D = 1024
DC = 8
FF = 2816
FC = 22
G = 11
S = 2048
NBLK = 16
NSEQ = 4
ALPHA = 2.0 ** 0.25
INV_ALPHA = 1.0 / ALPHA
EPS_LN = 1e-5 / (ALPHA * ALPHA)
EPS_RMS = 1e-5
LAMBDA_INIT = 0.8 - 0.6 * math.exp(-0.3 * 0)
UW = 2072
UOFF = 16
WSLOT = 4224
CONV_BATCH = 8


class Op:
    __slots__ = ("eng", "fn", "deps", "signal", "tick", "dma", "semval", "group")


class Sched:
    ENG = ("pe", "act", "dve", "pool", "sp")
    COMPUTE = ("pe", "act", "dve", "pool")

    def __init__(self):
        self.q = {e: [] for e in self.ENG}
        self.lastw = {}
        self.readers = {}
        self.dma_count = {}
        self.dma_last = {}
        self.last_compute = {}

    def add(self, eng, fn, reads=(), writes=(), dma=None, group=None):
        op = Op()
        op.eng = eng
        op.fn = fn
        op.signal = False
        op.tick = None
        op.dma = dma
        op.group = group
        op.semval = None
        deps = {}

        def adddep(d, raw=True):
            if d is op:
                return
            if group is not None and d.group == group:
                return
            if d.dma is None and d.eng == "pe" and eng == "pe" and dma is None:
                return
            if (not raw) and d.dma is None and dma is None and d.eng == eng and eng in ("act", "dve"):
                return
            deps[id(d)] = d

        if dma is not None:
            self.dma_count[dma] = self.dma_count.get(dma, 0) + 16
            op.semval = self.dma_count[dma]
            prev = self.dma_last.get(dma)
            if prev is not None:
                adddep(prev)
            self.dma_last[dma] = op
        for r in reads:
            w = self.lastw.get(r)
            if w is not None:
                adddep(w)
        for w_ in writes:
            w = self.lastw.get(w_)
            if w is not None:
                adddep(w, raw=False)
            rd = self.readers.get(w_)
            if rd:
                for k, d in rd.items():
                    if k == "dma":
                        for dd in d:
                            adddep(dd, raw=False)
                    else:
                        adddep(d, raw=False)
        op.deps = list(deps.values())
        for r in reads:
            rd = self.readers.setdefault(r, {})
            if dma is not None:
                rd.setdefault("dma", []).append(op)
            else:
                rd[eng] = op
        for w_ in writes:
            self.lastw[w_] = op
            self.readers[w_] = {}
        self.q[eng].append(op)
        if dma is None and fn is not None:
            self.last_compute[eng] = op
        return op

    def barrier(self):
        lasts = [self.last_compute[e] for e in self.COMPUTE if e in self.last_compute]
        for e in self.ENG:
            op = Op()
            op.eng = e
            op.fn = None
            op.signal = False
            op.tick = None
            op.dma = None
            op.group = None
            op.semval = None
            op.deps = [d for d in lasts if d.eng != e]
            self.q[e].append(op)
        self.lastw = {k: v for k, v in self.lastw.items() if v.dma is not None}
        newr = {}
        for k, rd in self.readers.items():
            if "dma" in rd and rd["dma"]:
                newr[k] = {"dma": rd["dma"]}
        self.readers = newr

    def finalize(self):
        for e in self.ENG:
            for op in self.q[e]:
                for d in op.deps:
                    if d.dma is None:
                        d.signal = True
        for e in self.ENG:
            t = 0
            for op in self.q[e]:
                if op.signal:
                    t += 1
                    op.tick = t

    def emit_stream(self, e, eng, esem, dsem):
        waited = {}
        for op in self.q[e]:
            need = {}
            for d in op.deps:
                if d.dma is not None:
                    key = ("d", d.dma)
                    val = d.semval
                else:
                    key = ("e", d.eng)
                    val = d.tick
                if need.get(key, 0) < val:
                    need[key] = val
            for key, val in need.items():
                if waited.get(key, 0) < val:
                    sem = dsem[key[1]] if key[0] == "d" else esem[key[1]]
                    eng.wait_ge(sem, val)
                    waited[key] = val
            if op.fn is not None:
                ins = op.fn(eng)
                if op.dma is not None:
                    ins.then_inc(dsem[op.dma], 16)
                elif op.signal:
                    ins.then_inc(esem[e], 1)


LN_GB_ENG = "pool"


def build_program(nseq=NSEQ, stop_after=None, dbg=(), ffn_stage=4):
    nc = bass.Bass("TRN2", target_bir_lowering=False)
    sch = Sched()
    ntok = nseq * S

    def din(name, shape, dt=F32):
        return nc.dram_tensor(name, list(shape), dt, kind="ExternalInput").ap()

    x_d = din("x", [ntok, D])
    ln_g = [din("ln%d_g" % i, [1, D]) for i in (1, 2, 3)]
    ln_b = [din("ln%d_b" % i, [1, D]) for i in (1, 2, 3)]
    wg_d = [din("ffn%d_w_gate" % i, [D, FF]) for i in (1, 2)]
    wu_d = [din("ffn%d_w_up" % i, [D, FF]) for i in (1, 2)]
    wd_d = [din("ffn%d_w_down" % i, [FF, D]) for i in (1, 2)]
    win_d = din("w_in", [D, 4096])
    lam_d = [din(n, [1, 64]) for n in ("lambda_q1", "lambda_k1", "lambda_q2", "lambda_k2")]
    subg_d = din("attn_subln_g", [1, 128])
    poolw_d = din("pool_w", [4, 128, 128])
    pscale_d = din("pool_scale", [1, 512])
    wba_d = din("w_branch_att", [512, D])
    wbp_d = din("w_branch_pool", [512, D])
    wout_d = din("w_out", [D, D])
    cos_d = din("c_cos", [128, NBLK, 32])
    sin_d = din("c_sin", [128, NBLK, 32])
    identf_d = din("c_identf", [128, 128])
    identb_d = din("c_identb", [128, 128], BF16)
    ptbl_d = din("c_ptbl", [128, 4, 16])
    y_d = nc.dram_tensor("y", [ntok, D], F32, kind="ExternalOutput").ap()

    dbg_out = {}

    def dbg_tensor(name, shape, dt):
        t = nc.dram_tensor("dbg_" + name, list(shape), dt, kind="ExternalOutput").ap()
        dbg_out[name] = t
        return t

    def scr(name, shape):
        return nc.dram_tensor(name, list(shape), BF16).ap()

    wgu_s = [scr("wgu_s%d" % i, [FC // 2, 128, 2, DC, 256]) for i in (1, 2)]
    wd_s = [scr("wd_s%d" % i, [2, 128, G, 1024]) for i in (1, 2)]
    wqkv_s = scr("wqkv_s", [3, 128, DC, 512])
    wu_in_s = scr("wu_in_s", [4, 128, DC, 128])
    wgate_s = scr("wgate_s", [16, 128, DC, 128])
    wba_s = scr("wba_s", [128, 4, 1024])
    wbp_s = scr("wbp_s", [128, 4, 1024])
    wout_s = scr("wout_s", [128, 8, 1024])

    es = ExitStack()
    with es:
        def sb(name, shape, dt):
            return es.enter_context(nc.sbuf_tensor(name, list(shape), dt))

        x1buf = sb("x1buf", [128, NBLK, D], F32)
        arena = sb("arena", [128, 24704], BF16)
        arena4 = sb("arena4", [128, 8704], F32)
        ypoolT = sb("ypoolT", [128, 4, S], BF16)
        arena3 = sb("arena3", [128, 4096], BF16)
        wsl = sb("wsl", [128, 2, WSLOT], BF16)
        gb = sb("gb", [128, 2, D], F32)
        identf = sb("identf", [128, 128], F32)
        identb = sb("identb", [128, 128], BF16)
        cosT = sb("cosT", [128, NBLK, 32], F32)
        sinT = sb("sinT", [128, NBLK, 32], F32)
        g08 = sb("g08", [128, 128], F32)
        poolw = sb("poolw", [128, 4, 128], BF16)
        pscale = sb("pscale", [128, 4], F32)
        ptbl = sb("ptbl", [128, 4, 16], F32)
        lamt = sb("lamt", [128, 4, 64], F32)
        small = sb("small", [128, 128], F32)
        mv = sb("mv", [128, 8, 2], F32)
        stats = sb("stats", [128, 8, 12], F32)

        ps = [es.enter_context(nc.psum_tensor("ps%d" % i, [128, 512], F32)) for i in range(8)]

        actT = arena[:, 0:G * 1024].rearrange("p (f t) -> p f t", f=G)
        wdv = arena[:, G * 1024:2 * G * 1024].rearrange("p (f d) -> p f d", f=G)
        qT = arena[:, 0:8192].rearrange("p (h t) -> p h t", h=4)
        kT = arena[:, 8192:16384].rearrange("p (h t) -> p h t", h=4)
        vaug = arena[:, 16384:16384 + 16 * 4 * 130].rearrange("p (k h e) -> p k h e", k=16, h=4)
        wgA = arena[:, 8192:16384].rearrange("p (c k f) -> p c k f", c=8, k=DC)
        wgB = arena[:, 0:8192].rearrange("p (c k f) -> p c k f", c=8, k=DC)

        def wgate_chunk(att_pool, mc):
            w = wgA if mc < 4 else wgB
            return w[:, 4 * att_pool + (mc % 4), :, :], ("wgA" if mc < 4 else "wgB")

        wg_calls = [0]

        def load_wgate_half(hf, extra_writes=()):
            wg_calls[0] += 1
            gid = ("wg", wg_calls[0])
            w = wgA if hf == 0 else wgB
            key = "wgA" if hf == 0 else "wgB"
            sem = "cw1" if hf == 0 else "cw3"
            for ap in range(2):
                c0 = 8 * ap + 4 * hf
                dma(w[:, 4 * ap:4 * ap + 4, :, :].rearrange("p c k f -> p c (k f)"),
                    wgate_s[c0:c0 + 4].rearrange("c p k f -> p c (k f)"), sem, scr_keys("cv_c"),
                    [key] + list(extra_writes), group=gid)
        woutv = arena[:, 16384:24576].rearrange("p (m d) -> p m d", m=8)

        uT = arena4[:, 0:4 * UW].rearrange("p (g t) -> p g t", g=4)
        a4b = arena4[:, :].bitcast(BF16)
        xt16 = a4b[:, 0:8192].rearrange("p (c t) -> p c t", c=DC)
        silu_tmp = arena4[:, 4096:5120].rearrange("p (a t) -> p a t", a=2)
        yattT = a4b[:, 0:8192].rearrange("p (h t) -> p h t", h=4)
        mergedT = a4b[:, 8192:12288].rearrange("p (m t) -> p m t", m=8)
        sigtmp = arena4[:, 6144:8192].rearrange("p (a t) -> p a t", a=4)

        xt8 = arena3[:, :].rearrange("p (c t) -> p c t", c=DC)
        ptile = arena3[:, :].rearrange("p (a t) -> p a t", a=8)
        pooled = arena3[:, 0:S]

        wsl_f32 = [wsl[:, i, :].bitcast(F32) for i in range(2)]

        gbf = gb[:, :, :].rearrange("p a d -> p (a d)")
        gbb = gbf.bitcast(BF16)
        ypf = ypoolT[:, :, :].rearrange("p a t -> p (a t)")
        ropeA = ypf[:, 0:512].bitcast(F32).rearrange("p (g i) -> p g i", g=8)
        ropeB = ypf[:, 512:1024].bitcast(F32).rearrange("p (g i) -> p g i", g=8)
        qrope = [ypf[:, 1024 + 512 * i:1024 + 512 * (i + 1)] for i in range(4)]
        ocopy = gbf[:, 0:1032].rearrange("p (b c) -> p b c", b=4)
        sqbuf = gbf[:, 0:512].rearrange("p (q e) -> p q e", q=4)
        obuf = gbf[:, 1032:1544].rearrange("p (q e) -> p q e", q=4)
        yatt_tok = gbb[:, 3088:3600].rearrange("p (q e) -> p q e", q=4)

        neglam = small[:, 0:1]
        eps_ln = small[:, 1:2]
        eps_rms = small[:, 2:3]
        lsum = small[:, 4:6]
        lexp = small[:, 6:8]
        rec = small[:, 8:16]
        nr2 = small[:, 16:20]
        ssq = small[:, 20:24]
        lnr = small[:, 24:28]
        rinv = small[:, 28:32]
        lnv = small[:, 32:40]
        rstd = small[:, 40:48]
        nmr = small[:, 48:56]
        ptmp = small[:, 64:80]

        def psb(i):
            return ps[i][:, :].bitcast(BF16)

        def mm(out, lhsT, rhs, start, stop, reads, writes, skip=False):
            sch.add("pe", lambda e: e.matmul(out, lhsT, rhs, start=start, stop=stop,
                                             skip_group_check=skip), reads, writes)

        def tr(out, in_, ident, reads, writes):
            sch.add("pe", lambda e: e.transpose(out, in_, ident), reads, writes)

        def act(out, in_, func, reads, writes, bias=None, scale=None):
            kw = {}
            if bias is not None:
                kw["bias"] = bias
            if scale is not None:
                kw["scale"] = scale
            sch.add("act", lambda e: e.activation(out=out, in_=in_, func=func, **kw), reads, writes)

        def vcopy(engname, out, in_, reads, writes):
            if engname == "act":
                act(out, in_, AF.Copy, reads, writes)
            else:
                sch.add(engname, lambda e: e.tensor_copy(out, in_), reads, writes)

        def tt(out, in0, in1, op, reads, writes, eng="dve"):
            sch.add(eng, lambda e: e.tensor_tensor(out, in0, in1, op), reads, writes)

        def ts(out, in0, s1, s2, op0, op1, reads, writes, eng="dve"):
            if op1 is None:
                sch.add(eng, lambda e: e.tensor_scalar(out, in0, s1, None, op0), reads, writes)
            else:
                sch.add(eng, lambda e: e.tensor_scalar(out, in0, s1, s2, op0, op1), reads, writes)

        def stt(out, in0, scalar, in1, op0, op1, reads, writes):
            sch.add("dve", lambda e: e.scalar_tensor_tensor(out, in0, scalar, in1, op0, op1),
                    reads, writes)

        def memset(eng, ap, val, reads, writes):
            sch.add(eng, lambda e: e.memset(ap, val), reads, writes)

        def dma(out, in_, sem, reads, writes, group=None, q="sp", nonc=False):
            if nonc:
                sch.add(q, lambda e: e.dma_start(out=out, in_=in_, allow_slow_non_contiguous=True),
                        reads, writes, dma=sem, group=group)
            else:
                sch.add(q, lambda e: e.dma_start(out=out, in_=in_), reads, writes, dma=sem, group=group)

        conv_cnt = {}
        conv_batches = []

        def conv(out, in_, grp):
            n = conv_cnt.get(grp, 0)
            conv_cnt[grp] = n + 1
            key = (grp, n // CONV_BATCH)
            if n % CONV_BATCH == 0:
                if len(conv_batches) >= 1:
                    sch.add("pool", None, (), ())
                    sch.q["pool"][-1].deps = [conv_batches[-1][1]]
                conv_batches.append([key, None])
            sch.add("pool", lambda e: e.dma_start(out=out, in_=in_), (), [("scr",) + key], dma=key, group=key)
            conv_batches[-1][1] = sch.q["pool"][-1]

        def scr_keys(grp):
            return [("scr", grp, k) for k in range((conv_cnt[grp] + CONV_BATCH - 1) // CONV_BATCH)]

        def conv_ffn(i, half):
            wg_v = wg_d[i].rearrange("(c p) (n f) -> n p c f", p=128, f=256)
            wu_v = wu_d[i].rearrange("(c p) (n f) -> n p c f", p=128, f=256)
            pairs = range(0, 6) if half == 0 else range(6, 11)
            for pp in pairs:
                grp = "cv_f%d_q%d" % (i + 1, pp // 2)
                conv(wgu_s[i][pp, :, 0, :, :], wg_v[pp], grp)
                conv(wgu_s[i][pp, :, 1, :, :], wu_v[pp], grp)
            grp = "cv_f%d_d%d" % (i + 1, half)
            for j in range(G):
                fc = half * G + j
                conv(wd_s[i][half, :, j, :], wd_d[i][fc * 128:(fc + 1) * 128, :], grp)

        dma(identf[:, :], identf_d, "c0", (), ["identf"])
        dma(identb[:, :], identb_d, "c1", (), ["identb"])
        dma(cosT[:, :, :], cos_d, "c2", (), ["cos"])
        dma(sinT[:, :, :], sin_d, "c3", (), ["sin"])
        dma(ptbl[:, :, :], ptbl_d, "c4", (), ["ptbl"])
        for i in range(4):
            dma(lamt[:, i, :], lam_d[i].broadcast_to([128, 64]), "c5", (), ["lamt"], group="lamt")
        dma(g08[:, :], subg_d.broadcast_to([128, 128]), "c6", (), ["g08"])
        dma(pscale[:, :], pscale_d.rearrange("o (g d) -> d (o g)", g=4), "c7", (), ["pscale"], nonc=True)

        conv_ffn(0, 0)
        dma(poolw[:, :, :], poolw_d.rearrange("g c d -> c g d"), "c8", (), ["poolw"], q="pool")
        conv_ffn(0, 1)
        for g3 in range(3):
            conv(wqkv_s[g3], win_d[:, g3 * 512:(g3 + 1) * 512].rearrange("(c p) n -> p c n", p=128), "cv_in")
        uview = win_d[:, 1536:2048].rearrange("(c p) (n f) -> n p c f", p=128, f=128)
        for ch in range(4):
            conv(wu_in_s[ch], uview[ch], "cv_in")
        gview = win_d[:, 2048:4096].rearrange("(c p) (n f) -> n p c f", p=128, f=128)
        for ch in range(16):
            conv(wgate_s[ch], gview[ch], "cv_c")
        conv(wba_s, wba_d.rearrange("(k p) m -> p k m", p=128), "cv_c")
        conv(wbp_s, wbp_d.rearrange("(k p) m -> p k m", p=128), "cv_c")
        conv(wout_s, wout_d.rearrange("(k p) m -> p k m", p=128), "cv_c")
        conv_ffn(1, 0)
        conv_ffn(1, 1)

        memset("dve", eps_ln, EPS_LN, (), ["eps"])
        memset("dve", eps_rms, EPS_RMS, (), ["eps"])
        tt(lamt[:, 0, :], lamt[:, 0, :], lamt[:, 1, :], ALU.mult, ["lamt"], ["lamt"])
        tt(lamt[:, 2, :], lamt[:, 2, :], lamt[:, 3, :], ALU.mult, ["lamt"], ["lamt"])
        sch.add("dve", lambda e: e.tensor_reduce(lsum[:, 0:1], lamt[:, 0, :], AX.X, ALU.add), ["lamt"], ["lsum"])
        sch.add("dve", lambda e: e.tensor_reduce(lsum[:, 1:2], lamt[:, 2, :], AX.X, ALU.add), ["lamt"], ["lsum"])
        act(lexp, lsum, AF.Exp, ["lsum"], ["lexp"])
        tt(neglam, lexp[:, 1:2], lexp[:, 0:1], ALU.subtract, ["lexp"], ["neglam"])
        ts(neglam, neglam, -LAMBDA_INIT, None, ALU.add, None, ["neglam"], ["neglam"])
        ts(g08[:, :], g08[:, :], 1.0 - LAMBDA_INIT, None, ALU.mult, None, ["g08"], ["g08"])

        tcount = [0]

        def transposes_f32(bi, dst, dstkey, col0):
            b0 = (tcount[0] % 2) * 2
            tcount[0] += 1
            for dc in range(DC):
                bank = b0 + dc // 4
                tr(ps[bank][:, (dc % 4) * 128:(dc % 4 + 1) * 128], x1buf[:, bi, dc * 128:(dc + 1) * 128],
                   identf[:, :], [("x1", bi), "identf"], [("ps", bank)])
            for hb in range(2):
                engn = "dve" if hb == 0 else "act"
                vcopy(engn, dst[:, hb * 4:(hb + 1) * 4, col0:col0 + 128],
                      ps[b0 + hb][:, :].rearrange("p (c t) -> p c t", c=4),
                      [("ps", b0 + hb)], [(dstkey, col0 // 128, hb)])

        def load_gb(idx):
            dma(gb[:, 0, :], ln_g[idx].broadcast_to([128, D]), "gb", (), ["gb"], group=("gb", idx, tcount[0]))
            dma(gb[:, 1, :], ln_b[idx].broadcast_to([128, D]), "gb", (), ["gb"], group=("gb", idx, tcount[0]))

        def layernorm_gen(blocks, after=None, exposed=False, after2=None):
            n = len(blocks)
            for j, bi in enumerate(blocks):
                sch.add("dve", lambda e, j=j, bi=bi: e.bn_stats(stats[:, j, 0:6], x1buf[:, bi, 0:512]),
                        [("x1", bi)], [("st", j)])
                sch.add("dve", lambda e, j=j, bi=bi: e.bn_stats(stats[:, j, 6:12], x1buf[:, bi, 512:1024]),
                        [("x1", bi)], [("st", j)])
                sch.add("dve", lambda e, j=j: e.bn_aggr(mv[:, j, :], stats[:, j, :]), [("st", j)], [("mv", j)])
                act(lnv[:, j:j + 1], mv[:, j, 1:2], AF.Ln, [("mv", j), "eps"], [("lnv", j)], bias=eps_ln, scale=1.0)
                act(rstd[:, j:j + 1], lnv[:, j:j + 1], AF.Exp, [("lnv", j)], [("rstd", j)], scale=-0.5)
                stt(nmr[:, j:j + 1], mv[:, j, 0:1], -1.0, rstd[:, j:j + 1], ALU.mult, ALU.mult,
                    [("mv", j), ("rstd", j)], [("nmr", j)])
                yield
            for j, bi in enumerate(blocks):
                act(x1buf[:, bi, :], x1buf[:, bi, :], AF.Identity, [("x1", bi), ("rstd", j), ("nmr", j)],
                    [("x1", bi)], bias=nmr[:, j:j + 1], scale=rstd[:, j:j + 1])
                if exposed:
                    engn = "pool" if (j % 8) in (1, 3, 6) else "dve"
                else:
                    engn = "pool"
                tt(x1buf[:, bi, :], x1buf[:, bi, :], gb[:, 0, :], ALU.mult, [("x1", bi), "gb"], [("x1", bi)],
                   eng=engn)
                tt(x1buf[:, bi, :], x1buf[:, bi, :], gb[:, 1, :], ALU.add, [("x1", bi), "gb"], [("x1", bi)],
                   eng=engn)
                yield
            if after is not None:
                after()
            if after2 is not None:
                for _ in range(5):
                    yield
                after2()

        def drain(gen, k=None):
            if gen is None:
                return None
            try:
                if k is None:
                    while True:
                        next(gen)
                else:
                    for _ in range(k):
                        next(gen)
            except StopIteration:
                return None
            return gen

        wpair = [0]

        def ffn_load_pair(which, pp):
            si = wpair[0] % 2
            wpair[0] += 1
            rds = scr_keys("cv_f%d_q%d" % (which + 1, pp // 2))
            dma(wsl[:, si, 0:4096], wgu_s[which][pp].rearrange("p g c f -> p (g c f)"),
                ("ws", si), rds, [("ws", si)])
            return si

        def load_x(seq, T):
            r0 = seq * S + T * 1024
            for hf in range(2):
                b0_ = T * 8 + 4 * hf
                dma(x1buf[:, b0_:b0_ + 4, :],
                    x_d[r0 + 512 * hf:r0 + 512 * (hf + 1), :].rearrange("(j p) d -> p j d", p=128),
                    "xin%d" % hf, (), [("x1", b0_ + j) for j in range(4)])

        def ffn_tile(seq, T, which, pending=None, after_ln=None, exposed=False, early=None):
            blocks = [T * 8 + j for j in range(8)]
            r0 = seq * S + T * 1024
            slots = {}
            slots[0] = ffn_load_pair(which, 0)
            slots[1] = ffn_load_pair(which, 1)

            for j, bi in enumerate(blocks):
                transposes_f32(bi, xt16, "xt16", j * 128)
            par = 0
            if ffn_stage < 2:
                return
            for gi in range(2):
                grp = "cv_f%d_d%d" % (which + 1, gi)
                dma(wdv, wd_s[which][gi], "wd", scr_keys(grp), ["wd"])
                for fl in range(G):
                    fc = gi * G + fl
                    pp, a = fc // 2, fc % 2
                    si = slots[pp]
                    wv = wsl[:, si, 0:4096].rearrange("p (g c a f) -> p a g c f", g=2, c=DC, a=2)
                    for half in range(2):
                        gbank, ubank = 4 + 2 * par, 5 + 2 * par
                        xk = [("xt16", half * 4 + jb, hb) for jb in range(4) for hb in range(2)]
                        for gu, bank in ((0, gbank), (1, ubank)):
                            for dc in range(DC):
                                mm(ps[bank][:, :], wv[:, a, gu, dc, :], xt16[:, dc, half * 512:(half + 1) * 512],
                                   dc == 0, dc == DC - 1, [("ws", si)] + xk, [("ps", bank)])
                        act(silu_tmp[:, par, :], ps[gbank][:, :], AF.Silu, [("ps", gbank)], [("silu", par)])
                        tt(actT[:, fl, half * 512:(half + 1) * 512], ps[ubank][:, :], silu_tmp[:, par, :], ALU.mult,
                           [("ps", ubank), ("silu", par)], [("act", fl, half)])
                        par ^= 1
                    if a == 1 and pp + 2 <= (FC // 2) - 1:
                        slots[pp + 2] = ffn_load_pair(which, pp + 2)
                    pending = drain(pending, 1)
                for r in range(4 if ffn_stage >= 3 else 0):
                    for jj in range(2):
                        j = 2 * r + jj
                        bi = blocks[j]
                        for dh in range(2):
                            bank = (r % 2) * 4 + jj * 2 + dh
                            for fl in range(G):
                                mm(ps[bank][:, :], actT[:, fl, j * 128:(j + 1) * 128],
                                   wdv[:, fl, dh * 512:(dh + 1) * 512], fl == 0, fl == G - 1,
                                   [("act", fl, j // 4), "wd"], [("ps", bank)])
                            stt(x1buf[:, bi, dh * 512:(dh + 1) * 512], ps[bank][:, :], 0.5 * INV_ALPHA,
                                x1buf[:, bi, dh * 512:(dh + 1) * 512], ALU.mult, ALU.add,
                                [("ps", bank), ("x1", bi)], [("x1", bi)])
            pending = drain(pending)
            if T == 0:
                load_gb(0 if which == 0 else 2)

            def after():
                if which == 1:
                    for hf in range(2):
                        b0_ = T * 8 + 4 * hf
                        dma(y_d[r0 + 512 * hf:r0 + 512 * (hf + 1), :].rearrange("(j p) d -> p j d", p=128),
                            x1buf[:, b0_:b0_ + 4, :], "yout%d" % hf, [("x1", b0_ + j) for j in range(4)],
                            [("yout", hf)])
            return layernorm_gen(blocks, after, exposed, after_ln)

        def phase_a2(seq, pending=None):
            memset("dve", vaug[:, :, :, 128:130], 1.0, (), ["vones"])
            memset("dve", uT[:, :, 0:UOFF], 0.0, (), ["upad"])
            memset("dve", uT[:, :, UOFF + S:UW], 0.0, (), ["upad"])
            ropepar = [0]
            pend_a2 = [pending]
            for tq in range(4):
                if tq == 2:
                    pend_a2[0] = drain(pend_a2[0])
                for jb in range(4):
                    transposes_f32(tq * 4 + jb, xt8, "xt8", jb * 128)
                xk_all = [("xt8", jb, hb) for jb in range(4) for hb in range(2)]
                wvs = []
                for qk in range(2):
                    dma(wsl[:, qk, 0:4096], wqkv_s[qk].rearrange("p c n -> p (c n)"), ("ws", qk),
                        scr_keys("cv_in"), [("ws", qk)])
                    wvs.append(wsl[:, qk, 0:4096].rearrange("p (c n) -> p c n", c=DC))
                rps = {}

                def qk_proj(qk, jb):
                    bi = tq * 4 + jb
                    bank = 4 + 2 * qk + (jb % 2)
                    for dc in range(DC):
                        mm(ps[bank][:, :], xt8[:, dc, jb * 128:(jb + 1) * 128], wvs[qk][:, dc, :], dc == 0,
                           dc == DC - 1, [("ws", qk), ("xt8", jb, 0), ("xt8", jb, 1)], [("ps", bank)])
                    rp = ropepar[0] % 4
                    ropepar[0] += 1
                    rps[(qk, jb)] = rp
                    src = ps[bank][:, :].rearrange("p (g t i) -> p g t i", g=8, t=2)
                    dst = qrope[rp].rearrange("p (g t i) -> p g t i", g=8, t=2)
                    cb = cosT[:, bi, :].unsqueeze(1).to_broadcast([128, 8, 32])
                    sbb = sinT[:, bi, :].unsqueeze(1).to_broadcast([128, 8, 32])
                    pk = [("ps", bank)]
                    tt(ropeA, src[:, :, 0, :], cb, ALU.mult, pk + ["cos"], ["ropeA"])
                    tt(ropeB, src[:, :, 1, :], sbb, ALU.mult, pk + ["sin"], ["ropeB"])
                    tt(dst[:, :, 0, :], ropeA, ropeB, ALU.subtract, ["ropeA", "ropeB"], [("qrope", rp)])
                    tt(ropeA, src[:, :, 0, :], sbb, ALU.mult, pk + ["sin"], ["ropeA"])
                    tt(ropeB, src[:, :, 1, :], cb, ALU.mult, pk + ["cos"], ["ropeB"])
                    tt(dst[:, :, 1, :], ropeA, ropeB, ALU.add, ["ropeA", "ropeB"], [("qrope", rp)])

                def qk_tr(qk, jb):
                    rp = rps[(qk, jb)]
                    tb0 = 0 if qk == 0 else 2
                    for h in range(4):
                        tbank = tb0 + h // 2
                        c0 = (h % 2) * 512 + jb * 128
                        tr(psb(tbank)[:, c0:c0 + 128], qrope[rp][:, h * 128:(h + 1) * 128], identb[:, :],
                           [("qrope", rp), "identb"], [("ps", tbank)])

                def qk_evac(qk):
                    dstT = qT if qk == 0 else kT
                    tb0 = 0 if qk == 0 else 2
                    for h in range(4):
                        tbank = tb0 + h // 2
                        vcopy("dve" if h // 2 == 0 else "act", dstT[:, h, tq * 512:(tq + 1) * 512],
                              psb(tbank)[:, (h % 2) * 512:(h % 2 + 1) * 512], [("ps", tbank)], [("qkT", qk, h)])

                qk_proj(0, 0)
                qk_proj(1, 0)
                for jb in range(1, 4):
                    qk_proj(0, jb)
                    qk_proj(1, jb)
                    qk_tr(0, jb - 1)
                    qk_tr(1, jb - 1)
                tail_qk = [lambda: (qk_tr(0, 3), qk_tr(1, 3), qk_evac(0), qk_evac(1))]
                si = 0
                dma(wsl[:, si, 0:4096], wqkv_s[2].rearrange("p c n -> p (c n)"), ("ws", si),
                    scr_keys("cv_in"), [("ws", si)])
                wv = wsl[:, si, 0:4096].rearrange("p (c n) -> p c n", c=DC)
                for jb in range(4):
                    pend_a2[0] = drain(pend_a2[0], 1)
                    bi = tq * 4 + jb
                    bank = 4 + (jb % 2)
                    for dc in range(DC):
                        mm(ps[bank][:, :], xt8[:, dc, jb * 128:(jb + 1) * 128], wv[:, dc, :], dc == 0, dc == DC - 1,
                           [("ws", si), ("xt8", jb, 0), ("xt8", jb, 1)], [("ps", bank)])
                    vcopy("act" if jb % 2 == 0 else "dve", vaug[:, bi, :, 0:128],
                          ps[bank][:, :].rearrange("p (h e) -> p h e", h=4), [("ps", bank)], [("v", bi)])
                si = 1
                dma(wsl[:, si, 0:4096].rearrange("p (c x) -> p c x", c=4), wu_in_s.rearrange("c p k f -> p c (k f)"),
                    ("ws", si), scr_keys("cv_in"), [("ws", si)])
                wv = wsl[:, si, 0:4096].rearrange("p (c k f) -> p c k f", c=4, k=DC)
                for ch in range(4):
                    pend_a2[0] = drain(pend_a2[0], 1)
                    bank = 6 + (ch % 2)
                    for dc in range(DC):
                        mm(ps[bank][:, :], wv[:, ch, dc, :], xt8[:, dc, :], dc == 0, dc == DC - 1,
                           [("ws", si)] + xk_all, [("ps", bank)])
                    vcopy("dve" if ch % 2 == 0 else "act", uT[:, ch, UOFF + tq * 512:UOFF + (tq + 1) * 512],
                          ps[bank][:, :], [("ps", bank), "upad"], [("u", ch)])
                    if ch == 0:
                        tail_qk[0]()
            Ta, Tb = wsl_f32[0], wsl_f32[1]
            xk_all = [("xt8", jb, hb) for jb in range(4) for hb in range(2)]
            for g4 in range(4):
                k = g4 + 1
                w = 1 << k
                U = uT[:, g4, :]
                bufs = [Ta, Tb]
                keys = [("ws", 0), ("ws", 1)]
                srcb, srck = U, ("u", g4)
                for lv in range(k):
                    sh = 1 << lv
                    lo = 2 * sh - 1
                    dstb, dstk = bufs[lv % 2], keys[lv % 2]
                    tt(dstb[:, lo:UW], srcb[:, lo:UW], srcb[:, lo - sh:UW - sh], ALU.add, [srck, "upad"], [dstk])
                    srcb, srck = dstb, dstk
                off = UOFF + w // 2 - 1
                stt(pooled[:, 0:S], srcb[:, off:off + S], 1.0 / w, U[:, UOFF:UOFF + S], ALU.mult, ALU.subtract,
                    [srck, ("u", g4)], xk_all + ["pooled"])
                nl = w // 2
                tt(ptmp[:, 0:nl], srcb[:, off:off + nl], ptbl[:, g4, 0:nl], ALU.mult, [srck, "ptbl"], ["ptmp"])
                tt(pooled[:, 0:nl], ptmp[:, 0:nl], U[:, UOFF:UOFF + nl], ALU.subtract, ["ptmp", ("u", g4)], ["pooled"])
                nr = w // 2 - 1
                if nr > 0:
                    tt(ptmp[:, 0:nr], srcb[:, off + S - nr:off + S], ptbl[:, g4, 8:8 + nr], ALU.mult,
                       [srck, "ptbl"], ["ptmp"])
                    tt(pooled[:, S - nr:S], ptmp[:, 0:nr], U[:, UOFF + S - nr:UOFF + S], ALU.subtract,
                       ["ptmp", ("u", g4)], ["pooled"])
                for tq in range(4):
                    bank = 4 + (tq % 2)
                    mm(ps[bank][:, :], poolw[:, g4, :], pooled[:, tq * 512:(tq + 1) * 512], True, True,
                       ["pooled", "poolw"], [("ps", bank)])
                    act(ypoolT[:, g4, tq * 512:(tq + 1) * 512], ps[bank][:, :], AF.Identity, [("ps", bank), "pscale"],
                        [("ypool", g4), "ropeA", "ropeB"] + [("qrope", i) for i in range(4)],
                        scale=pscale[:, g4:g4 + 1])

        def phase_b(seq):
            sbank = [0]
            pt = [0]
            pending = [None]

            def emit_S(h, qt, kc, m):
                sb_ = 4 + (sbank[0] % 3)
                sbank[0] += 1
                pi = pt[0] % 8
                pt[0] += 1
                kz = wsl[:, h % 2, 0:4096].rearrange("p (m t) -> p m t", m=2)
                mm(ps[sb_][:, :], kz[:, m, kc * 128:(kc + 1) * 128],
                   qT[:, h, qt * 512:(qt + 1) * 512], True, True,
                   [("qkT", 0, h), ("kz", h % 2)], [("ps", sb_)])
                act(ptile[:, pi, :], ps[sb_][:, :], AF.Exp, [("ps", sb_)], [("pt", pi)], scale=0.125)
                return pi

            def emit_PV(h, kc, m, pi):
                for qb in range(4):
                    ob = 2 * m + qb // 2
                    c0 = (qb % 2) * 129
                    mm(ps[ob][:, c0:c0 + 129], ptile[:, pi, qb * 128:(qb + 1) * 128],
                       vaug[:, kc, h, 0:129], (kc == 0 and qb % 2 == 0), kc == 15,
                       [("pt", pi), ("v", kc), "vones"], [("ps", ob)], skip=True)

            def emit_post(h, qt):
                for ob in range(4):
                    vcopy("dve", ocopy[:, ob, :], ps[ob][:, 0:258], [("ps", ob)], [("oc", ob)])
                ocf = gbf[:, 0:1032]
                allo = [("oc", ob) for ob in range(4)]
                sch.add("dve", lambda e: e.reciprocal(rec.rearrange("p (b j) -> p b j", b=4),
                                                      ocopy[:, :, 128:258:129]), allo, ["rec"])
                ts(nr2, rec[:, 4:8], neglam, None, ALU.mult, None, ["rec", "neglam"], ["nr2"])
                o1v = ocf[:, 0:516].rearrange("p (q e) -> p q e", q=4)[:, :, 0:128]
                o2v = ocf[:, 516:1032].rearrange("p (q e) -> p q e", q=4)[:, :, 0:128]
                tt(obuf, o1v, rec[:, 0:4].unsqueeze(2).to_broadcast([128, 4, 128]), ALU.mult, allo + ["rec"],
                   [("obuf", qb) for qb in range(4)])
                tt(o2v, o2v, nr2.unsqueeze(2).to_broadcast([128, 4, 128]), ALU.mult, allo + ["nr2"], allo)
                tt(obuf, obuf, o2v, ALU.add, allo + [("obuf", qb) for qb in range(4)],
                   [("obuf", qb) for qb in range(4)])
                ok = [("obuf", qb) for qb in range(4)]
                tt(sqbuf, obuf, obuf, ALU.mult, ok, ["sq"] + [("oc", ob) for ob in range(4)])
                sch.add("dve", lambda e: e.tensor_reduce(ssq, sqbuf, AX.X, ALU.add),
                        ["sq"] + [("oc", ob) for ob in range(4)], ["ssq"])
                def fin2():
                    act(lnr, ssq, AF.Ln, ["ssq", "eps"], ["lnr"], bias=eps_rms, scale=1.0 / 128.0)
                    act(rinv, lnr, AF.Exp, ["lnr"], ["rinv"], scale=-0.5)
                    for qb in range(4):
                        stt(yatt_tok[:, qb, :], obuf[:, qb, :], rinv[:, qb:qb + 1], g08[:, :], ALU.mult, ALU.mult,
                            [("obuf", qb), "rinv", "g08"], [("yat", qb)])

                def fin():
                    for qb in range(4):
                        tr(psb(7)[:, qb * 128:(qb + 1) * 128], yatt_tok[:, qb, :], identb[:, :],
                           [("yat", qb), "identb"], [("ps", 7)])
                    vcopy("dve", yattT[:, h, qt * 512:(qt + 1) * 512], psb(7)[:, 0:512], [("ps", 7)],
                          [("yattT", h)])
                return fin2, fin

            for sl in range(2):
                kzs = wsl[:, sl, 0:4096].rearrange("p (m t) -> p m t", m=2)
                memset("dve", kzs[64:128, 0, :], 0.0, (), [("kz", sl)])
                memset("dve", kzs[0:64, 1, :], 0.0, (), [("kz", sl)])
            def kz_copies(h):
                kzv = wsl[:, h % 2, 0:4096].rearrange("p (m t) -> p m t", m=2)
                sch.add("dve", lambda e: e.tensor_copy(kzv[0:64, 0, :], kT[0:64, h, :]),
                        [("qkT", 1, h)], [("kz", h % 2)])
                sch.add("dve", lambda e: e.tensor_copy(kzv[64:128, 1, :], kT[64:128, h, :]),
                        [("qkT", 1, h)], [("kz", h % 2)])
                if h == 3:
                    load_wgate_half(0, [("qkT", 1, hh) for hh in range(4)])

            kz_copies(0)
            kz_copies(1)
            for h in range(4):
                for qt in range(4):
                    steps = [(kc, m) for kc in range(16) for m in range(2)]
                    pis = {}
                    for i in range(2):
                        pis[i] = emit_S(h, qt, steps[i][0], steps[i][1])
                    for i in range(32):
                        if i + 2 < 32:
                            pis[i + 2] = emit_S(h, qt, steps[i + 2][0], steps[i + 2][1])
                        emit_PV(h, steps[i][0], steps[i][1], pis[i])
                        if i == 4 and qt == 1 and 1 <= h <= 2:
                            kz_copies(h + 1)
                        if i == 4 and qt == 1 and h == 3:
                            dma(wsl[:, 0, 0:4096].rearrange("p (k m) -> p k m", k=4), wba_s, ("ws", 0),
                                scr_keys("cv_c"), [("ws", 0), ("kz", 0)])
                        if i == 14 and pending[0] is not None:
                            pending[0][0]()
                        if i == 24 and pending[0] is not None:
                            pending[0][1]()
                            pending[0] = None
                    pending[0] = emit_post(h, qt)
            if pending[0] is not None:
                pending[0][0]()
                pending[0][1]()
                pending[0] = None

        def phase_c(seq):
            load_wgate_half(1)
            dma(woutv, wout_s, "cw2", scr_keys("cv_c"), ["wout"])
            dma(wsl[:, 1, 0:4096].rearrange("p (k m) -> p k m", k=4), wbp_s, ("ws", 1), scr_keys("cv_c"), [("ws", 1)])
            load_gb(1)
            wba = wsl[:, 0, 0:4096].rearrange("p (k m) -> p k m", k=4)
            wbp = wsl[:, 1, 0:4096].rearrange("p (k m) -> p k m", k=4)
            par = 0
            pend_c = [None]
            for tq in range(4):
                if tq == 0:
                    for jb in range(4):
                        transposes_f32(jb, xt8, "xt8", jb * 128)
                xk_all = [("xt8", jb, hb) for jb in range(4) for hb in range(2)]
                tsl = slice(tq * 512, (tq + 1) * 512)
                import os
                cst = int(os.environ.get("C_STAGE", "9"))
                for mc in range(8 if cst >= 2 else 0):
                    b_a, b_p, b_ga, b_gp = 4 * par, 4 * par + 1, 4 * par + 2, 4 * par + 3
                    for kc in range(4):
                        mm(ps[b_a][:, :], wba[:, kc, mc * 128:(mc + 1) * 128], yattT[:, kc, tsl], kc == 0, kc == 3,
                           [("ws", 0), ("yattT", kc)], [("ps", b_a)])
                    for kc in range(4):
                        mm(ps[b_p][:, :], wbp[:, kc, mc * 128:(mc + 1) * 128], ypoolT[:, kc, tsl], kc == 0, kc == 3,
                           [("ws", 1), ("ypool", kc)], [("ps", b_p)])
                    wga_, wk = wgate_chunk(0, mc)
                    wgp_, _ = wgate_chunk(1, mc)
                    for dc in range(DC):
                        mm(ps[b_ga][:, :], wga_[:, dc, :], xt8[:, dc, :], dc == 0, dc == DC - 1,
                           [wk] + xk_all, [("ps", b_ga)])
                    for dc in range(DC):
                        mm(ps[b_gp][:, :], wgp_[:, dc, :], xt8[:, dc, :], dc == 0, dc == DC - 1,
                           [wk] + xk_all, [("ps", b_gp)])
                    sa, sp_ = sigtmp[:, 2 * par, :], sigtmp[:, 2 * par + 1, :]
                    act(sa, ps[b_ga][:, :], AF.Sigmoid, [("ps", b_ga)], [("sig", 2 * par)])
                    act(sp_, ps[b_gp][:, :], AF.Sigmoid, [("ps", b_gp)], [("sig", 2 * par + 1)])
                    tt(sa, ps[b_a][:, :], sa, ALU.mult, [("ps", b_a), ("sig", 2 * par)], [("sig", 2 * par)])
                    tt(sp_, ps[b_p][:, :], sp_, ALU.mult, [("ps", b_p), ("sig", 2 * par + 1)], [("sig", 2 * par + 1)])
                    tt(mergedT[:, mc, :], sa, sp_, ALU.add, [("sig", 2 * par), ("sig", 2 * par + 1)], [("mrg", mc)])
                    par ^= 1
                    pend_c[0] = drain(pend_c[0], 2)
                if tq < 3:
                    for jb in range(4):
                        transposes_f32((tq + 1) * 4 + jb, xt8, "xt8", jb * 128)
                blocks = [tq * 4 + jb for jb in range(4)]
                for jb, bi in enumerate(blocks if cst >= 3 else []):
                    for dh in range(2):
                        bank = (jb % 2) * 2 + dh + 4 * par
                        for mc in range(8):
                            mm(ps[bank][:, :], mergedT[:, mc, jb * 128:(jb + 1) * 128],
                               woutv[:, mc, dh * 512:(dh + 1) * 512], mc == 0, mc == 7,
                               [("mrg", mc), "wout"], [("ps", bank)])
                        stt(x1buf[:, bi, dh * 512:(dh + 1) * 512], ps[bank][:, :], INV_ALPHA,
                            x1buf[:, bi, dh * 512:(dh + 1) * 512], ALU.mult, ALU.add,
                            [("ps", bank), ("x1", bi)], [("x1", bi)])
                pend_c[0] = drain(pend_c[0])
                pend_c[0] = layernorm_gen(blocks, None, False)
            return pend_c[0]

        def dump(name, ap, shape, dt):
            t = dbg_tensor(name, shape, dt)
            sch.barrier()
            dma(t, ap, "dbg_" + name, (), ["dbgout"])

        done = False
        if stop_after == "p0":
            dump("g08", g08[:, :], [128, 128], F32)
            dma(y_d[0:128, :], x_d[0:128, :], "yout", (), ["yout"])
        else:
            load_x(0, 0)
            load_x(0, 1)
        carry = None
        for seq in range(nseq if stop_after != "p0" else 0):
            g0 = ffn_tile(seq, 0, 0, pending=carry)
            carry = None
            g1 = ffn_tile(seq, 1, 0, pending=g0)
            if stop_after == "a1" or (seq == 0 and "x1" in dbg):
                g1 = drain(g1)
            if seq == 0 and "x1" in dbg:
                dump("x1", x1buf[:, :, :], [128, NBLK, D], F32)
            if stop_after == "a1":
                break
            sch.barrier()
            phase_a2(seq, pending=g1)
            if seq == 0 and "a2" in dbg:
                dump("qT", qT, [128, 4, S], BF16)
                dump("kT", kT, [128, 4, S], BF16)
                dump("vaug", vaug, [128, 16, 4, 130], BF16)
                dump("ypoolT", ypoolT[:, :, :], [128, 4, S], BF16)
            if stop_after == "a2":
                break
            sch.barrier()
            phase_b(seq)
            if seq == 0 and "b" in dbg:
                dump("yattT", yattT, [128, 4, S], BF16)
            if stop_after == "b":
                break
            sch.barrier()
            gc = phase_c(seq)
            if stop_after == "c" or (seq == 0 and "c" in dbg):
                gc = drain(gc)
            if seq == 0 and "c" in dbg:
                dump("x2", x1buf[:, :, :], [128, NBLK, D], F32)
            if stop_after == "c":
                break
            sch.barrier()
            nxt0 = (lambda sq=seq: load_x(sq + 1, 0)) if seq + 1 < nseq else None
            nxt1 = (lambda sq=seq: load_x(sq + 1, 1)) if seq + 1 < nseq else None
            g0 = ffn_tile(seq, 0, 1, pending=gc, after_ln=nxt0)
            carry = ffn_tile(seq, 1, 1, pending=g0, after_ln=nxt1)
        drain(carry)

        sch.barrier()
        fin = [op for op in sch.dma_last.values()]
        op = Op()
        op.eng, op.fn, op.signal, op.tick, op.dma, op.group, op.semval = "sp", None, False, None, None, None, None
        op.deps = fin
        sch.q["sp"].append(op)

        sch.finalize()

        esem = {e: es.enter_context(nc.semaphore("sem_" + e)) for e in Sched.COMPUTE}
        dsem = {}
        for i, k in enumerate(sch.dma_count.keys()):
            dsem[k] = es.enter_context(nc.semaphore("dsem%d" % i))
        block = es.enter_context(nc.Block())

        @block.tensor
        def _(eng):
            sch.emit_stream("pe", eng, esem, dsem)

        @block.scalar
        def _(eng):
            sch.emit_stream("act", eng, esem, dsem)

        @block.vector
        def _(eng):
            sch.emit_stream("dve", eng, esem, dsem)

        @block.gpsimd
        def _(eng):
            sch.emit_stream("pool", eng, esem, dsem)

        @block.sync
        def _(eng):
            sch.emit_stream("sp", eng, esem, dsem)

    return nc, dbg_out


def _constants():
    import ml_dtypes
    inv = 1.0 / (10000.0 ** (np.arange(0, 64, 2, dtype=np.float64) / 64.0))
    pos = np.arange(S, dtype=np.float64)
    ang = (pos[:, None].astype(np.float32) * inv[None, :].astype(np.float32)).astype(np.float32)
    cos = np.cos(ang.astype(np.float64)).astype(np.float32)
    sin = np.sin(ang.astype(np.float64)).astype(np.float32)
    cosT = np.ascontiguousarray(cos.reshape(NBLK, 128, 32).transpose(1, 0, 2))
    sinT = np.ascontiguousarray(sin.reshape(NBLK, 128, 32).transpose(1, 0, 2))
    identf = np.eye(128, dtype=np.float32)
    identb = np.eye(128, dtype=np.float32).astype(ml_dtypes.bfloat16)
    tbl = np.zeros((4, 16), dtype=np.float32)
    for g in range(4):
        w = 2 << g
        for t in range(w // 2):
            tbl[g, t] = 1.0 / (t + w // 2)
        nr = w // 2 - 1
        for i in range(nr):
            t = S - nr + i
            tbl[g, 8 + i] = 1.0 / (S - t + w // 2)
    ptbl = np.ascontiguousarray(np.broadcast_to(tbl[None], (128, 4, 16))).astype(np.float32)
    return {"c_cos": cosT, "c_sin": sinT, "c_identf": identf, "c_identb": identb, "c_ptbl": ptbl}


_WNAMES = ["ln1_g", "ln1_b", "ffn1_w_gate", "ffn1_w_up", "ffn1_w_down", "w_in", "lambda_q1", "lambda_k1",
           "lambda_q2", "lambda_k2", "attn_subln_g", "pool_w", "pool_scale", "w_branch_att", "w_branch_pool",
           "w_out", "ln2_g", "ln2_b", "ffn2_w_gate", "ffn2_w_up", "ffn2_w_down", "ln3_g", "ln3_b"]


def _prep_weights(inputs):
    out = {}
    for n in _WNAMES:
        a = np.asarray(inputs[n], dtype=np.float32)
        out[n] = np.ascontiguousarray(a[0])
    out.update(_constants())
    return out


def kernel(**inputs):
    x = np.asarray(inputs["x"], dtype=np.float32)
    B = x.shape[0]
    per = B // NCORES
    wmap = _prep_weights(inputs)
    nc, _ = build_program(nseq=per)
    in_maps = []
    for c in range(NCORES):
        m = dict(wmap)
        m["x"] = np.ascontiguousarray(x[c * per:(c + 1) * per].reshape(per * S, D))
        in_maps.append(m)
    res = run_bass_kernel_spmd(nc, in_maps, core_ids=list(range(NCORES)))
    outs = [np.asarray(r["y"], dtype=np.float32).reshape(per, S, D) for r in res.results]
    return np.concatenate(outs, axis=0)
```
